# Optimizing a Trainium2 kernel written in Bass

```python
import math
import jax, jax.numpy as jnp
from jax import lax
import numpy as np

D_MODEL = 1024
BATCH = 4
SEQ = 4096
DEPTH = 2
DEC_BATCH = 32
DEC_SEQ = 4
PAST_LEN = 8192
PAGE_SIZE = 128

N_A_LAYERS = DEPTH // 2
N_B_LAYERS = DEPTH - N_A_LAYERS

SSM_INNER = 2 * D_MODEL
SSM_HEAD_DIM = 64
SSM_HEADS = SSM_INNER // SSM_HEAD_DIM
SSM_GROUPS = 4
SSM_HEADS_PER_GROUP = SSM_HEADS // SSM_GROUPS
SSM_STATE = 128
CONV_WIDTH = 4
CONV_DIM = SSM_INNER + 2 * SSM_GROUPS * SSM_STATE
SSM_CHUNK = 128
SSM_IN_DIM = SSM_INNER + CONV_DIM + SSM_HEADS

ATT_HEAD_DIM = 128
KV_HEADS = D_MODEL // ATT_HEAD_DIM
WINDOWS = (128, 512, 2048)
DILATIONS = (1, 4, 16)
N_DIL_GROUPS = len(WINDOWS)
Q_HEADS = N_DIL_GROUPS * KV_HEADS
Q_WIDTH = Q_HEADS * ATT_HEAD_DIM
ATT_WIDTH = KV_HEADS * ATT_HEAD_DIM
B_IN_DIM = Q_WIDTH + ATT_WIDTH
WINDOW_MAX = max(WINDOWS)
N_BUCKETS = 32
MAX_DISTANCE = WINDOW_MAX

NORM_EPS = 1e-5
NEG_INF = -1e30

kernel_name = "yoco_ssd_dilated_window_step"


def rmsnorm(x, g):
    xf = x.astype(jnp.float32)
    y = xf * lax.rsqrt(jnp.mean(xf * xf, axis=-1, keepdims=True) + NORM_EPS)
    return (y * g.astype(jnp.float32)).astype(x.dtype)


def t5_bucket(dist):
    max_exact = N_BUCKETS // 2
    n = jnp.maximum(dist, 0)
    nf = jnp.maximum(n, 1).astype(jnp.float32)
    large = max_exact + (jnp.log(nf / max_exact) / math.log(MAX_DISTANCE / max_exact)
                         * (N_BUCKETS - max_exact)).astype(jnp.int32)
    large = jnp.minimum(large, N_BUCKETS - 1)
    return jnp.where(n < max_exact, n, large)


def group_bias(rel_bias, g):
    nw = WINDOWS[g] // DILATIONS[g]
    dist = jnp.arange(nw + 1, dtype=jnp.int32) * DILATIONS[g]
    cols = rel_bias[:, g * KV_HEADS:(g + 1) * KV_HEADS].astype(jnp.float32)
    return cols[t5_bucket(dist)]


def ssd_scan(xdt, adt, bm, cm, init):
    b, l, nh, p = xdt.shape
    T = min(SSM_CHUNK, l)
    nc = -(-l // T)
    lp = nc * T

    def pad(t):
        return jnp.pad(t, ((0, 0), (0, lp - l)) + ((0, 0),) * (t.ndim - 2))

    G, J = SSM_GROUPS, SSM_HEADS_PER_GROUP
    X = pad(xdt).reshape(b, nc, T, G, J, p)
    A = pad(adt).reshape(b, nc, T, G, J).transpose(0, 3, 4, 1, 2)
    Bc = pad(bm).reshape(b, nc, T, G, SSM_STATE)
    Cc = pad(cm).reshape(b, nc, T, G, SSM_STATE)
    a_cs = jnp.cumsum(A, axis=-1)
    tri = jnp.tril(jnp.ones((T, T), dtype=bool))
    seg = jnp.exp(jnp.where(tri, a_cs[..., :, None] - a_cs[..., None, :], -jnp.inf))
    cb = jnp.einsum("bctgn,bcsgn->bcgts", Cc, Bc)
    m = jnp.einsum("bcgts,bgjcts->bcgjts", cb, seg)
    y_diag = jnp.einsum("bcgjts,bcsgjp->bctgjp", m, X)
    decay_in = jnp.exp(a_cs[..., -1:] - a_cs)
    chunk_states = jnp.einsum("bctgn,bgjct,bctgjp->cbgjpn", Bc, decay_in, X)
    chunk_decay = jnp.exp(a_cs[..., -1]).transpose(3, 0, 1, 2)
    s0 = init.reshape(b, G, J, p, SSM_STATE)

    def step(s, inp):
        dec, cs = inp
        return dec[..., None, None] * s + cs, s

    s_final, s_in = lax.scan(step, s0, (chunk_decay, chunk_states))
    y_off = jnp.einsum("bctgn,cbgjpn,bgjct->bctgjp", Cc, s_in, jnp.exp(a_cs))
    y = (y_diag + y_off).reshape(b, lp, nh, p)[:, :l]
    return y, s_final.reshape(b, nh, p, SSM_STATE)


def ssd_mixer(h, conv_prev, ssm_init, g_norm, w_in, conv_w, conv_b, dt_bias, a_log, d_skip,
              g_gate, w_out):
    b, l, _ = h.shape
    u = rmsnorm(h, g_norm) @ w_in
    z = u[..., :SSM_INNER]
    xbc = u[..., SSM_INNER:SSM_INNER + CONV_DIM]
    dt_raw = u[..., SSM_INNER + CONV_DIM:]
    xc = jnp.concatenate([conv_prev.astype(xbc.dtype), xbc], axis=1)
    conv = conv_b
    for k in range(CONV_WIDTH):
        conv = conv + xc[:, k:k + l] * conv_w[k]
    new_conv = xc[:, l:]
    act = jax.nn.silu(conv.astype(jnp.float32))
    xs = act[..., :SSM_INNER].reshape(b, l, SSM_HEADS, SSM_HEAD_DIM)
    gn = SSM_GROUPS * SSM_STATE
    bm = act[..., SSM_INNER:SSM_INNER + gn].reshape(b, l, SSM_GROUPS, SSM_STATE)
    cm = act[..., SSM_INNER + gn:].reshape(b, l, SSM_GROUPS, SSM_STATE)
    dt = jax.nn.softplus(dt_raw.astype(jnp.float32) + dt_bias.astype(jnp.float32))
    a = -jnp.exp(a_log.astype(jnp.float32))
    y, s_final = ssd_scan(xs * dt[..., None], dt * a, bm, cm, ssm_init.astype(jnp.float32))
    y = y + xs * d_skip.astype(jnp.float32)[:, None]
    y = rmsnorm(y.reshape(b, l, SSM_INNER) * jax.nn.silu(z.astype(jnp.float32)), g_gate)
    return y.astype(h.dtype) @ w_out, new_conv, s_final


def dilated_attn_prompt(q, k, v, bvec, dil, nw):
    b, s, h, dh = q.shape
    m = s // dil
    lb = nw
    nb = -(-m // lb)
    mp = nb * lb

    def to_sub(t):
        t = t.reshape(b, m, dil, h, dh).transpose(0, 2, 1, 3, 4)
        return jnp.pad(t, ((0, 0), (0, 0), (0, mp - m), (0, 0), (0, 0)))

    def key_blocks(t):
        t = jnp.pad(to_sub(t), ((0, 0), (0, 0), (lb, 0), (0, 0), (0, 0)))
        t = t.reshape(b, dil, nb + 1, lb, h, dh)
        return jnp.concatenate([t[:, :, :-1], t[:, :, 1:]], axis=3)

    qs = to_sub(q).reshape(b, dil, nb, lb, h, dh)
    kb = key_blocks(k)
    vb = key_blocks(v)
    scores = jnp.einsum("brnqhd,brnkhd->brnhqk", qs, kb) * (dh ** -0.5)
    rel = lb + jnp.arange(lb)[:, None] - jnp.arange(2 * lb)[None, :]
    band = (rel >= 0) & (rel <= nw)
    bias = bvec[jnp.clip(rel, 0, nw)].transpose(2, 0, 1)
    key_ok = (jnp.arange(nb)[:, None] - 1) * lb + jnp.arange(2 * lb)[None, :] >= 0
    valid = band[None, None] & key_ok[:, None, None, :]
    scores = jnp.where(valid, scores + bias, NEG_INF)
    lse = jax.nn.logsumexp(scores, axis=-1)
    p = jnp.exp(scores - lse[..., None])
    o = jnp.einsum("brnhqk,brnkhd->brnqhd", p, vb)
    o = o.reshape(b, dil, mp, h, dh)[:, :, :m].transpose(0, 2, 1, 3, 4).reshape(b, s, h, dh)
    lse = lse.transpose(0, 1, 2, 4, 3).reshape(b, dil, mp, h)[:, :, :m]
    lse = lse.transpose(0, 2, 1, 3).reshape(b, s, h)
    return o, lse


def dilated_attn_sample(q, kc, vc, bvec, dil, nw, n_old):
    t = q.shape[1]
    idx = n_old + jnp.arange(t)[:, None] - dil * jnp.arange(nw + 1)[None, :]
    valid = idx >= 0
    idx = jnp.maximum(idx, 0)
    kg = kc[:, idx]
    vg = vc[:, idx]
    scores = jnp.einsum("bthd,btjhd->bthj", q, kg) * (q.shape[-1] ** -0.5) + bvec.T[None, None]
    scores = jnp.where(valid[None, :, None, :], scores, NEG_INF)
    lse = jax.nn.logsumexp(scores, axis=-1)
    p = jnp.exp(scores - lse[..., None])
    o = jnp.einsum("bthj,btjhd->bthd", p, vg)
    return o, lse


def dilated_mixer(h, attend, biases, g_norm, w_in, w_out):
    b, l, _ = h.shape
    u = rmsnorm(h, g_norm) @ w_in
    q = u[..., :Q_WIDTH].reshape(b, l, N_DIL_GROUPS, KV_HEADS, ATT_HEAD_DIM).astype(jnp.float32)
    gate = u[..., Q_WIDTH:].astype(jnp.float32)
    outs, lses = [], []
    for g in range(N_DIL_GROUPS):
        o, lse = attend(q[:, :, g], biases[g], DILATIONS[g], WINDOWS[g] // DILATIONS[g])
        outs.append(o)
        lses.append(lse)
    w = jax.nn.softmax(jnp.stack(lses, axis=0), axis=0)
    o = jnp.sum(w[..., None] * jnp.stack(outs, axis=0), axis=0)
    y = o.reshape(b, l, ATT_WIDTH) * jax.nn.silu(gate)
    return y.astype(h.dtype) @ w_out


def shared_kv(h, g, w_kv):
    b, l, _ = h.shape
    return (rmsnorm(h, g) @ w_kv).reshape(b, l, 2, KV_HEADS, ATT_HEAD_DIM)


def setup_inputs(seed: int = 0) -> dict:
    key = jax.random.key(seed)
    ks = jax.random.split(key, 24)
    nrm = jax.random.normal
    f32 = jnp.float32
    n_old = min(WINDOW_MAX, PAST_LEN)
    dt0 = jnp.exp(jax.random.uniform(ks[0], (N_A_LAYERS, SSM_HEADS), f32)
                  * (math.log(0.1) - math.log(0.001)) + math.log(0.001))
    return {
        "x_prompt": nrm(ks[1], (BATCH, SEQ, D_MODEL), f32),
        "x_sample": nrm(ks[2], (DEC_BATCH, DEC_SEQ, D_MODEL), f32),
        "state_ssm": 0.5 * nrm(ks[3], (N_A_LAYERS, DEC_BATCH, SSM_HEADS, SSM_HEAD_DIM, SSM_STATE), f32),
        "state_conv": nrm(ks[4], (N_A_LAYERS, DEC_BATCH, CONV_WIDTH - 1, CONV_DIM), f32),
        "cache_kv": nrm(ks[5], (DEC_BATCH, n_old, 2, KV_HEADS, ATT_HEAD_DIM), f32),
        "a_norm": 1.0 + 0.02 * nrm(ks[6], (N_A_LAYERS, D_MODEL), f32),
        "a_w_in": nrm(ks[7], (N_A_LAYERS, D_MODEL, SSM_IN_DIM), f32) * D_MODEL ** -0.5,
        "a_conv_w": nrm(ks[8], (N_A_LAYERS, CONV_WIDTH, CONV_DIM), f32) * CONV_WIDTH ** -0.5,
        "a_conv_b": 0.02 * nrm(ks[9], (N_A_LAYERS, CONV_DIM), f32),
        "a_dt_bias": dt0 + jnp.log(-jnp.expm1(-dt0)),
        "a_A_log": jnp.log(jax.random.uniform(ks[10], (N_A_LAYERS, SSM_HEADS), f32, 1.0, 16.0)),
        "a_D": 1.0 + 0.02 * nrm(ks[11], (N_A_LAYERS, SSM_HEADS), f32),
        "a_gate_norm": 1.0 + 0.02 * nrm(ks[12], (N_A_LAYERS, SSM_INNER), f32),
        "a_w_out": nrm(ks[13], (N_A_LAYERS, SSM_INNER, D_MODEL), f32) * SSM_INNER ** -0.5,
        "rel_bias": 0.5 * nrm(ks[14], (N_BUCKETS, Q_HEADS), f32),
        "kv_norm": 1.0 + 0.02 * nrm(ks[15], (D_MODEL,), f32),
        "w_kv": nrm(ks[16], (D_MODEL, 2 * ATT_WIDTH), f32) * D_MODEL ** -0.5,
        "b_norm": 1.0 + 0.02 * nrm(ks[17], (N_B_LAYERS, D_MODEL), f32),
        "b_w_in": nrm(ks[18], (N_B_LAYERS, D_MODEL, B_IN_DIM), f32) * D_MODEL ** -0.5,
        "b_w_out": nrm(ks[19], (N_B_LAYERS, ATT_WIDTH, D_MODEL), f32) * ATT_WIDTH ** -0.5,
        "final_norm": 1.0 + 0.02 * nrm(ks[20], (D_MODEL,), f32),
    }


def reference(x_prompt, x_sample, state_ssm, state_conv, cache_kv, a_norm, a_w_in, a_conv_w,
              a_conv_b, a_dt_bias, a_A_log, a_D, a_gate_norm, a_w_out, rel_bias, kv_norm, w_kv,
              b_norm, b_w_in, b_w_out, final_norm):
    hp, hs = x_prompt, x_sample
    bp, sp = x_prompt.shape[0], x_prompt.shape[1]
    n_old = cache_kv.shape[1]
    biases = [group_bias(rel_bias, g) for g in range(N_DIL_GROUPS)]
    ssm_p, ssm_s, conv_p, conv_s = [], [], [], []
    attend_p = attend_s = None
    kv_p = kv_s = None
    for layer in range(DEPTH):
        if layer < N_A_LAYERS:
            i = layer
            params = (a_norm[i], a_w_in[i], a_conv_w[i], a_conv_b[i], a_dt_bias[i], a_A_log[i],
                      a_D[i], a_gate_norm[i], a_w_out[i])
            conv0 = jnp.zeros((bp, CONV_WIDTH - 1, CONV_DIM), hp.dtype)
            ssm0 = jnp.zeros((bp, SSM_HEADS, SSM_HEAD_DIM, SSM_STATE), jnp.float32)
            o, c, s = ssd_mixer(hp, conv0, ssm0, *params)
            hp = hp + o
            conv_p.append(c)
            ssm_p.append(s)
            o, c, s = ssd_mixer(hs, state_conv[i], state_ssm[i], *params)
            hs = hs + o
            conv_s.append(c)
            ssm_s.append(s)
            if layer == N_A_LAYERS - 1:
                kv_p = shared_kv(hp, kv_norm, w_kv)
                kv_s = shared_kv(hs, kv_norm, w_kv)
                k_p = kv_p[:, :, 0].astype(jnp.float32)
                v_p = kv_p[:, :, 1].astype(jnp.float32)
                kc = jnp.concatenate([cache_kv[:, :, 0].astype(jnp.float32),
                                      kv_s[:, :, 0].astype(jnp.float32)], axis=1)
                vc = jnp.concatenate([cache_kv[:, :, 1].astype(jnp.float32),
                                      kv_s[:, :, 1].astype(jnp.float32)], axis=1)
                attend_p = (lambda q, bv, d, nw, k_p=k_p, v_p=v_p:
                            dilated_attn_prompt(q, k_p, v_p, bv, d, nw))
                attend_s = (lambda q, bv, d, nw, kc=kc, vc=vc:
                            dilated_attn_sample(q, kc, vc, bv, d, nw, n_old))
        else:
            j = layer - N_A_LAYERS
            hp = hp + dilated_mixer(hp, attend_p, biases, b_norm[j], b_w_in[j], b_w_out[j])
            hs = hs + dilated_mixer(hs, attend_s, biases, b_norm[j], b_w_in[j], b_w_out[j])
    y_prompt = rmsnorm(hp, final_norm)
    y_sample = rmsnorm(hs, final_norm)
    ssm_prompt = jnp.stack(ssm_p, axis=0).astype(state_ssm.dtype)
    ssm_sample = jnp.stack(ssm_s, axis=0).astype(state_ssm.dtype)
    conv_prompt = jnp.stack(conv_p, axis=0).astype(state_conv.dtype)
    conv_sample = jnp.stack(conv_s, axis=0).astype(state_conv.dtype)
    kv_prompt = kv_p[:, sp - min(WINDOW_MAX, sp):].astype(cache_kv.dtype)
    kv_sample = kv_s.astype(cache_kv.dtype)
    return (y_prompt, y_sample, ssm_prompt, ssm_sample, conv_prompt, conv_sample, kv_prompt, kv_sample)
```

```python
import math
import numpy as np
import concourse.bass as bass
import concourse.mybir as mybir
from concourse.bass_utils import run_bass_kernel_spmd
from contextlib import ExitStack

F32 = mybir.dt.float32
BF16 = mybir.dt.bfloat16
AF = mybir.ActivationFunctionType
ALU = mybir.AluOpType
AX = mybir.AxisListType

N_CORES = 8
D = 1024
SEQ = 4096
DI = 2048
CD = 3072
NH = 32
HP = 64
NS = 128
NG = 4
IN_DIM = 5152
NCHP = SEQ // 128
NSB = 4
NCH = NCHP + NSB
NST = NCH // 4
LTOT = NCH * 128
EPS = 1e-5
N_DMA_SEMS = 96
NWBLK = 28


class Sched:
    def __init__(self, nc, es):
        self.nc = nc
        self.engs = {"pe": nc.tensor, "act": nc.scalar, "dve": nc.vector, "pool": nc.gpsimd, "sp": nc.sync}
        self.sem = {}
        self.cnt = {}
        for k in self.engs:
            self.sem[k] = es.enter_context(nc.semaphore("sem_" + k))
            self.cnt[k] = 0
        self.dsem = [es.enter_context(nc.semaphore("dsem%d" % i)) for i in range(N_DMA_SEMS)]
        self.dcnt = [0] * N_DMA_SEMS
        self.dnext = 0
        self.seen = {k: {} for k in self.engs}
        self.last_w = {}
        self.readers = {}
        self.n_inst = 0

    def _semh(self, key):
        return self.sem[key] if isinstance(key, str) else self.dsem[key]

    def _wait(self, E, stamps):
        eng = self.engs[E]
        best = {}
        for (k, v) in stamps:
            if E == "pe" and k == "pe":
                continue
            if self.seen[E].get(k, 0) >= v:
                continue
            if best.get(k, 0) < v:
                best[k] = v
        for k, v in best.items():
            eng.wait_ge(self._semh(k), v)
            self.seen[E][k] = v

    def _deps(self, reads, writes, E=None):
        deps = []
        for r in reads:
            if r in self.last_w:
                deps.append(self.last_w[r])
            if r[:2] in ("pf", "pb"):
                deps.extend(st for st in self.readers.get(r, ()) if st[0] != E)
        for w in writes:
            if w in self.last_w:
                deps.append(self.last_w[w])
            deps.extend(self.readers.get(w, ()))
        return deps

    def _commit(self, stamp, reads, writes):
        for r in reads:
            self.readers.setdefault(r, []).append(stamp)
        for w in writes:
            self.last_w[w] = stamp
            self.readers[w] = []

    def op(self, E, fn, reads=(), writes=()):
        reads = [r for r in reads if r is not None]
        writes = [w for w in writes if w is not None]
        self._wait(E, self._deps(reads, writes, E))
        ins = fn(self.engs[E])
        self.cnt[E] += 1
        ins.then_inc(self.sem[E], 1)
        self._commit((E, self.cnt[E]), reads, writes)
        self.n_inst += 1
        return ins

    def dma(self, Q, out, in_, reads=(), writes=(), **kw):
        reads = [r for r in reads if r is not None]
        writes = [w for w in writes if w is not None]
        i = self.dnext
        self.dnext = (self.dnext + 1) % N_DMA_SEMS
        deps = self._deps(reads, writes)
        if self.dcnt[i] > 0:
            deps.append((i, self.dcnt[i]))
        self._wait(Q, deps)
        ins = self.engs[Q].dma_start(out=out, in_=in_, **kw)
        self.dcnt[i] += 16
        ins.then_inc(self.dsem[i], 16)
        self._commit((i, self.dcnt[i]), reads, writes)
        self.n_inst += 1
        return ins

    def all_stamps(self):
        st = [(k, self.cnt[k]) for k in self.engs if self.cnt[k] > 0]
        st += [(i, self.dcnt[i]) for i in range(N_DMA_SEMS) if self.dcnt[i] > 0]
        return st

    def barrier(self):
        st = self.all_stamps()
        for E in self.engs:
            self._wait(E, [s for s in st if s[0] != E])
        self.last_w = {}
        self.readers = {}

    def finish(self, E="sp"):
        self._wait(E, [s for s in self.all_stamps() if s[0] != E])


class Ring:
    def __init__(self, es, nc, name, shape, dt, n, psum=False):
        mk = nc.psum_tensor if psum else nc.sbuf_tensor
        self.tiles = [es.enter_context(mk("%s%d" % (name, i), list(shape), dt)) for i in range(n)]
        self.names = ["%s%d" % (name, i) for i in range(n)]
        self.i = 0

    def next(self):
        t, nm = self.tiles[self.i], self.names[self.i]
        self.i = (self.i + 1) % len(self.tiles)
        return t, nm


class _Stop(Exception):
    pass


def build_program(st_list=None, dbg=False, final_state_at=None, stop=None, spans=(0, 1, 2)):
    _cnt = {}

    def chk(tag):
        _cnt[tag] = _cnt.get(tag, 0) + 1
        if stop is not None and stop.split(":")[0] == tag and _cnt[tag] >= int((stop + ":1").split(":")[1]):
            raise _Stop()
    if st_list is None:
        st_list = list(range(NST))
    if final_state_at is None:
        final_state_at = NCHP - 1
    nc = bass.Bass("TRN2", target_bir_lowering=False)

    def din(name, shape):
        return nc.dram_tensor(name, list(shape), F32, kind="ExternalInput").ap()

    def dout(name, shape):
        return nc.dram_tensor(name, list(shape), F32, kind="ExternalOutput").ap()

    xp = din("xp", [SEQ, D])
    xsm = din("xs", [NSB * 4, D])
    sssm = din("sssm", [NSB, DI, NS])
    sconv = din("sconv", [NSB, 3, CD])
    ckv = din("ckv", [NSB, 2048, 2048])
    a_norm = din("a_norm", [D])
    a_w_in = din("a_w_in", [D, IN_DIM])
    a_conv_w = din("a_conv_w", [4, CD])
    a_conv_b = din("a_conv_b", [CD])
    a_dt_bias = din("a_dt_bias", [NH])
    a_A_log = din("a_A_log", [NH])
    a_D = din("a_D", [NH])
    a_gate_norm = din("a_gate_norm", [DI])
    a_w_out = din("a_w_out", [DI, D])
    rel_bias = din("rel_bias", [32, 24])
    kv_norm = din("kv_norm", [D])
    w_kv = din("w_kv", [D, 2048])
    b_norm = din("b_norm", [D])
    b_w_in = din("b_w_in", [D, 4096])
    b_w_out = din("b_w_out", [D, D])
    final_norm = din("final_norm", [D])

    yp = dout("yp", [2048, D])
    ysm = dout("ys", [NSB * 4, D])
    ssm_p = dout("ssm_p", [DI, NS])
    ssm_s = dout("ssm_s", [NSB, DI, NS])
    conv_p = dout("conv_p", [3, CD])
    conv_s = dout("conv_s", [NSB, 3, CD])
    kv_p = dout("kv_p", [2048, 2048])
    kv_s = dout("kv_s", [NSB * 4, 2048])

    wscr = nc.dram_tensor("wscr", [NWBLK, 128, 4096], BF16).ap()
    hAscr = nc.dram_tensor("hAscr", [LTOT, D], F32).ap()
    kvT = nc.dram_tensor("kvT", [16, 128, LTOT], BF16).ap()
    fbias_t = nc.dram_tensor("fbias", [24, 383], F32)
    fbias = fbias_t.ap()
    fbig_t = nc.dram_tensor("fbig", [24, 128, 383], F32)
    fbig = fbig_t.ap()
    onehot = din("onehot", [32, 3 * 129])
    spanflag = din("spanflag", [128, 2])

    with ExitStack() as es0:
        S = Sched(nc, es0)
        E0 = es0.enter_context

        def sb(name, shape, dt=F32, es=es0):
            return es.enter_context(nc.sbuf_tensor(name, list(shape), dt))

        iot = sb("iot", [128, 128])
        ident_f = sb("ident_f", [128, 128])
        ident_b = sb("ident_b", [128, 128], BF16)
        Umask = sb("Umask", [128, 128])
        Lmask = sb("Lmask", [128, 128])
        ones_f = sb("ones_f", [128, 128])
        Lmask_b = sb("Lmask_b", [128, 128], BF16)
        Umask_b = sb("Umask_b", [128, 128], BF16)
        tmask = sb("tmask", [128, 1])
        S.op("pool", lambda e: e.iota(iot[:], [[1, 128]], channel_multiplier=-1, allow_small_or_imprecise_dtypes=True), writes=["iot"])
        S.op("dve", lambda e: e.tensor_single_scalar(out=ident_f[:], in_=iot[:], scalar=0.0, op=ALU.is_equal), reads=["iot"], writes=["ident_f"])
        S.op("dve", lambda e: e.tensor_single_scalar(out=ident_b[:], in_=iot[:], scalar=0.0, op=ALU.is_equal), reads=["iot"], writes=["ident_b"])
        S.op("dve", lambda e: e.tensor_single_scalar(out=Umask[:], in_=iot[:], scalar=0.0, op=ALU.is_ge), reads=["iot"], writes=["Umask"])
        S.op("dve", lambda e: e.tensor_single_scalar(out=Lmask[:], in_=iot[:], scalar=0.0, op=ALU.is_lt), reads=["iot"], writes=["Lmask"])
        S.op("dve", lambda e: e.memset(ones_f[:], 1.0), writes=["ones_f"])
        S.op("dve", lambda e: e.tensor_single_scalar(out=Lmask_b[:], in_=iot[:], scalar=0.0, op=ALU.is_lt), reads=["iot"], writes=["Lmask_b"])
        S.op("dve", lambda e: e.tensor_single_scalar(out=Umask_b[:], in_=iot[:], scalar=0.0, op=ALU.is_ge), reads=["iot"], writes=["Umask_b"])
        S.op("dve", lambda e: e.tensor_single_scalar(out=tmask[:], in_=iot[:, 0:1], scalar=-3.5, op=ALU.is_gt), reads=["iot"], writes=["tmask"])

        g_a = sb("g_a", [128, 8])
        g_kv = sb("g_kv", [128, 8])
        g_b = sb("g_b", [128, 8])
        g_gate = sb("g_gate", [128, 16])
        cw = sb("cw", [128, 24, 4])
        cb_ = sb("cb_", [128, 24])
        dtb_bc = sb("dtb_bc", [128, NH])
        aneg_bc = sb("aneg_bc", [128, NH])
        D_bc = sb("D_bc", [128, NH])
        wdt = sb("wdt", [128, 8, NH], BF16)
        with nc.allow_non_contiguous_dma(reason="tiny one-time parameter layouts"):
            S.dma("sp", g_a[:], a_norm.rearrange("(k p) -> p k", p=128), writes=["g_a"])
            S.dma("sp", g_kv[:], kv_norm.rearrange("(k p) -> p k", p=128), writes=["g_kv"])
            S.dma("sp", g_b[:], b_norm.rearrange("(k p) -> p k", p=128), writes=["g_b"])
            S.dma("sp", g_gate[:], a_gate_norm.rearrange("(k p) -> p k", p=128), writes=["g_gate"])
            for k in range(4):
                S.dma("sp", cw[:, :, k], a_conv_w[k].rearrange("(c p) -> p c", p=128), writes=["cw"])
            S.dma("sp", cb_[:], a_conv_b.rearrange("(c p) -> p c", p=128), writes=["cb_"])
            S.dma("sp", dtb_bc[:], a_dt_bias.partition_broadcast(128), writes=["dtb_bc"])
            S.dma("sp", aneg_bc[:], a_A_log.partition_broadcast(128), writes=["aneg_bc"])
            S.dma("sp", D_bc[:], a_D.partition_broadcast(128), writes=["D_bc"])
        S.op("act", lambda e: e.activation(out=aneg_bc[:], in_=aneg_bc[:], func=AF.Exp), reads=["aneg_bc"], writes=["aneg_bc"])
        S.op("dve", lambda e: e.tensor_scalar(out=aneg_bc[:], in0=aneg_bc[:], scalar1=-1.0, scalar2=None, op0=ALU.mult), reads=["aneg_bc"], writes=["aneg_bc"])

        def wsrc(w, c0, r0=0):
            return w[r0:r0 + 1024, c0:c0 + 512].rearrange("(k p) c -> p k c", p=128)

        def wdst(blk):
            return wscr[blk].rearrange("p (k c) -> p k c", c=512)

        S.dma("pool", wdt[:], a_w_in[:, 5120:5152].rearrange("(k p) c -> p k c", p=128), writes=["wdt"])
        cvt = []
        for j in range(6):
            cvt.append([(wdst(j), wsrc(a_w_in, 2048 + 512 * j))])
        for j in range(4):
            cvt.append([(wdst(6 + j), wsrc(a_w_in, 512 * j))])
        for hf in range(2):
            for kp in range(2):
                cvt.append([(wdst(10 + 2 * hf + kp), wsrc(a_w_out, 512 * hf, 1024 * kp))])
        for j in range(4):
            cvt.append([(wdst(14 + j), wsrc(w_kv, 512 * j))])
        for h in range(8):
            parts = []
            for q in range(4):
                c0 = (q * 8 + h) * 128 if q < 3 else 3072 + h * 128
                parts.append((wdst(18 + h)[:, :, q * 128:(q + 1) * 128], b_w_in[:, c0:c0 + 128].rearrange("(k p) c -> p k c", p=128)))
            cvt.append(parts)
        for hf in range(2):
            cvt.append([(wdst(26 + hf), wsrc(b_w_out, 512 * hf))])
        cstate = {"n": 0}
        CVT_DEPTH = 3

        def cvt_upto(blk):
            while cstate["n"] <= min(blk, NWBLK - 1):
                k = cstate["n"]
                rd = ["cvt%d" % (k - CVT_DEPTH)] if k >= CVT_DEPTH else []
                for (dst_, src_) in cvt[k]:
                    S.dma("pool", dst_, src_, reads=rd, writes=["wscr%d" % k, "cvt%d" % k])
                cstate["n"] += 1

        cvt_upto(3)

        wring = [sb("wring%d" % i, [128, 8, 512], BF16) for i in range(3)]
        wseq = []
        for st in st_list:
            wseq += list(range(0, 18))
        for sp in range(2):
            wseq += list(range(18, 28))
        wstate = {"i": 0, "issued": 0}

        def w_issue(idx):
            blk = wseq[idx]
            slot = idx % 3
            cvt_upto(blk + 3)
            S.dma("sp", wring[slot][:], wdst(blk), reads=["wscr%d" % blk], writes=["wring%d" % slot])

        def w_get():
            i = wstate["i"]
            while wstate["issued"] <= min(i + 1, len(wseq) - 1):
                w_issue(wstate["issued"])
                wstate["issued"] += 1
            wstate["i"] += 1
            return wring[i % 3], "wring%d" % (i % 3)

        pf = Ring(es0, nc, "pf", [128, 512], F32, 5, psum=True)
        pb = Ring(es0, nc, "pb", [128, 1024], BF16, 3, psum=True)

        junk = sb("junk", [128, 1024], BF16)
        ssr = Ring(es0, nc, "ssr", [128, 4], F32, 4)
        xsbr = Ring(es0, nc, "xsb", [128, D], BF16, 2)
        es1 = ExitStack()
        es3 = ExitStack()
        es3b = ExitStack()
        NEG = -30000.0
        bmax = sb("bmax", [128, 2])
        oh_t = sb("oh_t", [32, 3 * 129], es=es1)
        rb_t = sb("rb_t", [32, 24], es=es1)
        rbm = sb("rbm", [32, 4], es=es1)
        Fg_t = sb("Fg_t", [8, 383], es=es1)
        rep_t = sb("rep_t", [128, 383], es=es1)
        S.dma("sp", oh_t[:], onehot, writes=["oh_t"])
        S.dma("sp", rb_t[:], rel_bias, writes=["rb_t"])
        S.op("dve", lambda e: e.reduce_max(out=rbm[:, 0:1], in_=rb_t[:], axis=AX.X), reads=["rb_t"], writes=["rbm"])
        pq, pqn = pf.next()
        S.op("pe", lambda e: e.matmul(pq[0:1, 0:32], lhsT=rbm[:, 0:1], rhs=ident_f[0:32, 0:32], start=True, stop=True), reads=["rbm", "ident_f"], writes=[pqn])
        S.op("dve", lambda e: e.reduce_max(out=rbm[0:1, 1:2], in_=pq[0:1, 0:32], axis=AX.X), reads=[pqn, "rbm"], writes=["rbm"])
        pq2, pq2n = pf.next()
        S.op("pe", lambda e: e.matmul(pq2[:, 0:1], lhsT=ones_f[0:1, :], rhs=rbm[0:1, 1:2], start=True, stop=True), reads=["ones_f", "rbm"], writes=[pq2n])
        S.op("dve", lambda e: e.tensor_copy(out=bmax[:, 0:1], in_=pq2[:, 0:1]), reads=[pq2n], writes=["bmax"])
        for g in range(3):
            S.op("dve", lambda e: e.memset(Fg_t[:], NEG), writes=["Fg_t"])
            psg, psgn = pf.next()
            S.op("pe", lambda e: e.matmul(psg[0:8, 0:129], lhsT=rb_t[:, g * 8:(g + 1) * 8], rhs=oh_t[:, g * 129:(g + 1) * 129], start=True, stop=True),
                 reads=["rb_t", "oh_t"], writes=[psgn])
            S.op("dve", lambda e: e.tensor_copy(out=Fg_t[:, 127:256], in_=psg[0:8, 0:129]), reads=[psgn, "Fg_t"], writes=["Fg_t"])
            S.dma("sp", fbias[g * 8:(g + 1) * 8, :], Fg_t[:], reads=["Fg_t"], writes=["fbias"])
        with nc.allow_non_contiguous_dma(reason="replicate bias vectors (one-time)"):
            for gh in range(24):
                S.dma("sp", fbig[gh], fbias[gh].partition_broadcast(128), reads=["fbias"], writes=["fbig"])
        try:
            def sb1(name, shape, dt=F32):
                return sb(name, shape, dt, es=es1)

            x4 = sb1("x4", [128, 4, D])
            fT = sb1("fT", [128, 8, 512], BF16)
            rawr = Ring(es1, nc, "raw", [128, 4, 131], F32, 3)
            accr = Ring(es1, nc, "acc", [128, 4, 128], F32, 2)
            tails = sb1("tails", [128, 24, 3])
            csamp = sb1("csamp", [128, 24, NSB, 3])
            sconvT = sb1("sconvT", [128, 24, NSB, 3])
            xcT = sb1("xcT", [128, 24, 512], BF16)
            zs = sb1("zs", [128, 4, DI], BF16)
            smr = Ring(es1, nc, "smr", [128, 12, NH], F32, 4)
            ahlr = Ring(es1, nc, "ahl", [128, 2, NH], BF16, 4)
            Rr = Ring(es1, nc, "Rr", [128, 2, 8, 128], BF16, 3)
            Er = Ring(es1, nc, "Er", [128, 512], F32, 2)
            cbmr = Ring(es1, nc, "cbm", [128, 128], F32, 2)
            mTr = Ring(es1, nc, "mT", [128, 8, 128], BF16, 2)
            Xr = Ring(es1, nc, "Xg", [128, 512], BF16, 2)
            Xdr = Ring(es1, nc, "Xdg", [128, 512], BF16, 2)
            xsDr = Ring(es1, nc, "xsD", [128, 512], BF16, 2)
            Btr = Ring(es1, nc, "Bt", [128, 128], BF16, 2)
            ST = sb1("ST", [128, DI])
            STb = sb1("STb", [128, DI], BF16)
            ybuf = sb1("ybuf", [128, DI])
            ygb = sb1("ygb", [128, DI], BF16)
            ygT = sb1("ygT", [128, 16, 512], BF16)
            ss2r = Ring(es1, nc, "ss2r", [128, 4], F32, 4)
            hkb = sb1("hkb", [128, D], BF16)
            kvtr = Ring(es1, nc, "kvt", [128, 512], F32, 2)
            kvbr = Ring(es1, nc, "kvb", [128, 512], BF16, 2)
            kTst = sb1("kTst", [128, 16, 256], BF16)

            chk("alloc")
            S.op("pool", lambda e: e.memset(tails[:], 0.0), writes=["tails"])
            S.op("pool", lambda e: e.memset(ST[:], 0.0), writes=["ST0", "ST1", "ST2", "ST3"])
            S.op("pool", lambda e: e.memset(STb[:], 0.0), writes=["STb0", "STb1", "STb2", "STb3"])
            with nc.allow_non_contiguous_dma(reason="conv state rows -> feature-major (tiny)"):
                for b in range(NSB):
                    for r in range(3):
                        S.dma("sp", sconvT[:, :, b, r], sconv[b, r].rearrange("(c p) -> p c", p=128), writes=["sconvT"])

            def rms_T(src, src_name, gcol, gname, dst3, dst_name):
                ss, ssn = ssr.next()
                S.op("act", lambda e: e.activation(out=junk[:], in_=src, func=AF.Square, accum_out=ss[:, 0:1]),
                     reads=[src_name], writes=["junk", ssn])
                S.op("dve", lambda e: e.tensor_scalar(out=ss[:, 1:2], in0=ss[:, 0:1], scalar1=1.0 / D, scalar2=EPS, op0=ALU.mult, op1=ALU.add),
                     reads=[ssn], writes=[ssn])
                S.op("act", lambda e: e.activation(out=ss[:, 2:3], in_=ss[:, 1:2], func=AF.Ln), reads=[ssn], writes=[ssn])
                S.op("act", lambda e: e.activation(out=ss[:, 3:4], in_=ss[:, 2:3], func=AF.Exp, scale=-0.5), reads=[ssn], writes=[ssn])
                xb, xbn = xsbr.next()
                S.op("act", lambda e: e.activation(out=xb[:], in_=src, func=AF.Copy, scale=ss[:, 3:4]),
                     reads=[src_name, ssn], writes=[xbn])
                pt, ptn = pb.next()
                for k in range(8):
                    S.op("pe", lambda e, k=k: e.transpose(out=pt[:, k * 128:(k + 1) * 128], in_=xb[:, k * 128:(k + 1) * 128], identity=ident_b[:]),
                         reads=[xbn, "ident_b"], writes=[ptn])
                S.op("dve", lambda e: e.tensor_tensor(out=dst3, in0=pt[:].rearrange("p (k t) -> p k t", t=128),
                                                      in1=gcol.unsqueeze(2).to_broadcast([128, 8, 128]), op=ALU.mult),
                     reads=[ptn, gname], writes=[dst_name])
                return ss, ssn

            last_prompt_st = max([x for x in st_list if x < NST - 1], default=None)
            for st in st_list:
                is_samp = st == NST - 1
                for i in range(4):
                    c = st * 4 + i
                    xtn = "x4_%d" % i
                    if is_samp:
                        S.op("pool", lambda e: e.memset(x4[:, i, :], 0.0), writes=[xtn])
                        S.dma("sp", x4[0:4, i, :], xsm[i * 4:(i + 1) * 4, :], writes=[xtn])
                    else:
                        S.dma("sp", x4[:, i, :], xp[c * 128:(c + 1) * 128, :], writes=[xtn])
                    rms_T(x4[:, i, :], xtn, g_a[:], "g_a", fT[:, :, i * 128:(i + 1) * 128], "fT")
                chk("rms")
                for j in range(6):
                    wt, wtn = w_get()
                    for q in range(4):
                        cc = j * 4 + q
                        ps, psn = pf.next()
                        for k in range(8):
                            S.op("pe", lambda e, k=k: e.matmul(ps[:], lhsT=wt[:, k, q * 128:(q + 1) * 128], rhs=fT[:, k, :], start=(k == 0), stop=(k == 7)),
                                 reads=[wtn, "fT"], writes=[psn])
                        raw, rawn = rawr.next()
                        S.op("act", lambda e: e.activation(out=raw[:, :, 3:131], in_=ps[:].rearrange("p (i t) -> p i t", t=128), func=AF.Copy),
                             reads=[psn], writes=[rawn])
                        if is_samp:
                            S.op("pool", lambda e: e.tensor_copy(out=raw[:, :, 0:3], in_=sconvT[:, cc, :, :]), reads=["sconvT"], writes=[rawn])
                            S.op("pool", lambda e: e.tensor_copy(out=csamp[:, cc, :, :], in_=raw[:, :, 4:7]), reads=[rawn], writes=["csamp"])
                        else:
                            S.op("pool", lambda e: e.tensor_copy(out=raw[:, 0, 0:3], in_=tails[:, cc, :]), reads=["tails"], writes=[rawn])
                            S.op("pool", lambda e: e.tensor_copy(out=raw[:, 1:4, 0:3], in_=raw[:, 0:3, 128:131]), reads=[rawn], writes=[rawn])
                            S.op("pool", lambda e: e.tensor_copy(out=tails[:, cc, :], in_=raw[:, 3, 128:131]), reads=[rawn], writes=["tails"])
                        acc, accn = accr.next()
                        S.op("act", lambda e: e.activation(out=acc[:], in_=raw[:, :, 3:131], func=AF.Identity, scale=cw[:, cc, 3:4], bias=cb_[:, cc:cc + 1]),
                             reads=[rawn, "cw", "cb_"], writes=[accn])
                        for kk, eng in ((0, "dve"), (1, "dve"), (2, "dve")):
                            S.op(eng, lambda e, kk=kk: e.scalar_tensor_tensor(out=acc[:], in0=raw[:, :, kk:kk + 128], scalar=cw[:, cc, kk:kk + 1], in1=acc[:],
                                                                              op0=ALU.mult, op1=ALU.add), reads=[rawn, "cw", accn], writes=[accn])
                        S.op("act", lambda e: e.activation(out=xcT[:, cc, :].rearrange("p (i t) -> p i t", t=128), in_=acc[:], func=AF.Silu),
                             reads=[accn], writes=["xcT"])
                def conv_out(src_of_cc, srcname, n, dst2d, tag):
                    for q6 in range(6):
                        pcv, pcvn = pf.next()
                        for u in range(4):
                            S.op("pe", lambda e, u=u: e.transpose(out=pcv[0:n, u * 128:(u + 1) * 128], in_=src_of_cc(q6 * 4 + u), identity=ident_f[:]),
                                 reads=[srcname, "ident_f"], writes=[pcvn])
                        stg_, stgn_ = accr.next()
                        S.op("dve", lambda e: e.tensor_copy(out=stg_[0:n].rearrange("p i t -> p (i t)"), in_=pcv[0:n, 0:512]), reads=[pcvn], writes=[stgn_])
                        S.dma("sp", dst2d[:, q6 * 512:(q6 + 1) * 512], stg_[0:n].rearrange("p i t -> p (i t)"), reads=[stgn_], writes=["%s%d" % (tag, q6)])
                if st == last_prompt_st:
                    conv_out(lambda cc: tails[:, cc, :], "tails", 3, conv_p, "conv_p")
                if is_samp:
                    conv_out(lambda cc: csamp[:, cc, :, :].rearrange("p b r -> p (b r)"), "csamp", 12, conv_s.rearrange("b r c -> (b r) c"), "conv_s")
                chk("xbc")
                for j in range(4):
                    wt, wtn = w_get()
                    for i in range(4):
                        ps, psn = pf.next()
                        for k in range(8):
                            S.op("pe", lambda e, k=k: e.matmul(ps[:], lhsT=fT[:, k, i * 128:(i + 1) * 128], rhs=wt[:, k, :], start=(k == 0), stop=(k == 7)),
                                 reads=[wtn, "fT"], writes=[psn])
                        S.op("act", lambda e: e.activation(out=zs[:, i, j * 512:(j + 1) * 512], in_=ps[:], func=AF.Silu), reads=[psn], writes=["zs%d" % i])
                chk("z")
                wo = [None] * 4
                cctx = []
                for i in range(4):
                    c = st * 4 + i
                    sm, smn = smr.next()
                    dtr, Ah, cs, ctot, tmp, dec_in, dtdec, ea, decb, dt_ = (sm[:, q, :] for q in range(10))
                    ps, psn = pf.next()
                    for k in range(8):
                        S.op("pe", lambda e, k=k: e.matmul(ps[:, 0:NH], lhsT=fT[:, k, i * 128:(i + 1) * 128], rhs=wdt[:, k, :], start=(k == 0), stop=(k == 7)),
                             reads=["wdt", "fT"], writes=[psn])
                    S.op("dve", lambda e: e.tensor_tensor(out=dtr, in0=ps[:, 0:NH], in1=dtb_bc[:], op=ALU.add), reads=[psn, "dtb_bc"], writes=[smn])
                    S.op("dve", lambda e: e.scalar_tensor_tensor(out=tmp, in0=dtr, scalar=-1.0, in1=dtr, op0=ALU.mult, op1=ALU.max), reads=[smn], writes=[smn])
                    S.op("act", lambda e: e.activation(out=tmp, in_=tmp, func=AF.Exp, scale=-1.0), reads=[smn], writes=[smn])
                    S.op("dve", lambda e: e.tensor_scalar(out=tmp, in0=tmp, scalar1=1.0, scalar2=None, op0=ALU.add), reads=[smn], writes=[smn])
                    S.op("act", lambda e: e.activation(out=tmp, in_=tmp, func=AF.Ln), reads=[smn], writes=[smn])
                    S.op("dve", lambda e: e.scalar_tensor_tensor(out=dt_, in0=dtr, scalar=0.0, in1=tmp, op0=ALU.max, op1=ALU.add), reads=[smn], writes=[smn])
                    if is_samp:
                        S.op("dve", lambda e: e.tensor_scalar(out=dt_, in0=dt_, scalar1=tmask[:, 0:1], scalar2=None, op0=ALU.mult), reads=[smn, "tmask"], writes=[smn])
                    S.op("dve", lambda e: e.tensor_tensor(out=Ah, in0=dt_, in1=aneg_bc[:], op=ALU.mult), reads=[smn, "aneg_bc"], writes=[smn])
                    ps2, ps2n = pf.next()
                    S.op("pe", lambda e: e.matmul(ps2[:, 0:NH], lhsT=Umask[:], rhs=Ah, start=True, stop=True), reads=["Umask", smn], writes=[ps2n])
                    S.op("pe", lambda e: e.matmul(ps2[:, NH:2 * NH], lhsT=ones_f[:], rhs=Ah, start=True, stop=True), reads=["ones_f", smn], writes=[ps2n])
                    S.op("dve", lambda e: e.tensor_copy(out=cs, in_=ps2[:, 0:NH]), reads=[ps2n], writes=[smn])
                    S.op("dve", lambda e: e.tensor_tensor(out=dec_in, in0=ps2[:, NH:2 * NH], in1=cs, op=ALU.subtract), reads=[ps2n, smn], writes=[smn])
                    S.op("act", lambda e: e.activation(out=dec_in, in_=dec_in, func=AF.Exp), reads=[smn], writes=[smn])
                    S.op("act", lambda e: e.activation(out=decb, in_=ps2[:, NH:2 * NH], func=AF.Exp), reads=[ps2n], writes=[smn])
                    S.op("act", lambda e: e.activation(out=ea, in_=cs, func=AF.Exp), reads=[smn], writes=[smn])
                    S.op("dve", lambda e: e.tensor_tensor(out=dtdec, in0=dt_, in1=dec_in, op=ALU.mult), reads=[smn], writes=[smn])

                    ahl, ahln = ahlr.next()
                    S.op("dve", lambda e: e.tensor_copy(out=ahl[:, 0, :], in_=Ah), reads=[smn], writes=[ahln])
                    S.op("dve", lambda e: e.tensor_tensor(out=ctot, in0=Ah, in1=ahl[:, 0, :], op=ALU.subtract), reads=[smn, ahln], writes=[smn])
                    S.op("dve", lambda e: e.tensor_copy(out=ahl[:, 1, :], in_=ctot), reads=[smn], writes=[ahln])
                    cctx.append((sm, smn, ahl, ahln))

                def make_chunk(i):
                    c = st * 4 + i
                    sm, smn, ahl, ahln = cctx[i]
                    gR = {}
                    dtr, Ah, cs, ctot, tmp, dec_in, dtdec, ea, decb, dt_ = (sm[:, q, :] for q in range(10))

                    def state_load():
                        if is_samp:
                            S.dma("sp", ybuf[:].rearrange("q (j n) -> q j n", n=128), sssm[i].rearrange("(j q) n -> q j n", q=128), writes=["ybuf0", "ybuf1", "ybuf2", "ybuf3"])
                            for jj in range(4):
                                pst, pstn = pf.next()
                                for u in range(4):
                                    j2 = jj * 4 + u
                                    S.op("pe", lambda e, j2=j2, u=u: e.transpose(out=pst[:, u * 128:(u + 1) * 128], in_=ybuf[:, j2 * 128:(j2 + 1) * 128], identity=ident_f[:]),
                                         reads=["ybuf%d" % jj, "ident_f"], writes=[pstn])
                                S.op("dve", lambda e: e.tensor_copy(out=ST[:, jj * 512:(jj + 1) * 512], in_=pst[:]), reads=[pstn], writes=["ST%d" % jj])
                                S.op("act", lambda e: e.activation(out=STb[:, jj * 512:(jj + 1) * 512], in_=pst[:], func=AF.Copy), reads=[pstn], writes=["STb%d" % jj])

                        return None

                    tk = slice(i * 128, (i + 1) * 128)
                    gst = {}

                    def ssd_R(g):
                        Rg, Rgn = Rr.next()
                        for hl in range(2):
                            S.op("pool", lambda e, hl=hl: e.tensor_tensor(out=Rg[:, hl, :, :], in0=Umask_b[:].unsqueeze(1).to_broadcast([128, 8, 128]),
                                                                         in1=ahl[:, hl, g * 8:(g + 1) * 8].unsqueeze(2).to_broadcast([128, 8, 128]), op=ALU.mult),
                                 reads=["Umask_b", ahln], writes=[Rgn])
                        gR[g] = (Rg, Rgn)

                    def ssd_front(g):
                        pt, ptn = pb.next()
                        for u in range(4):
                            S.op("pe", lambda e, u=u: e.transpose(out=pt[:, u * 128:(u + 1) * 128], in_=xcT[:, g * 4 + u, tk], identity=ident_b[:]),
                                 reads=["xcT", "ident_b"], writes=[ptn])
                        S.op("pe", lambda e: e.transpose(out=pt[:, 512:640], in_=xcT[:, 16 + g, tk], identity=ident_b[:]), reads=["xcT", "ident_b"], writes=[ptn])
                        chk("g_pe")
                        Xg, Xgn = Xr.next()
                        Xd, Xdn = Xdr.next()
                        xsD, xsDn = xsDr.next()
                        Bt, Btn = Btr.next()
                        pt3 = pt[:, 0:512].rearrange("p (h d) -> p h d", d=HP)

                        def bc8(ap2):
                            return ap2[:, g * 8:(g + 1) * 8].unsqueeze(2).to_broadcast([128, 8, HP])
                        S.op("dve", lambda e: e.tensor_tensor(out=Xg[:].rearrange("p (h d) -> p h d", d=HP), in0=pt3, in1=bc8(dt_), op=ALU.mult), reads=[ptn, smn], writes=[Xgn])
                        chk("g_ev1")
                        S.op("dve", lambda e: e.tensor_tensor(out=Xd[:].rearrange("p (h d) -> p h d", d=HP), in0=pt3, in1=bc8(dtdec), op=ALU.mult), reads=[ptn, smn], writes=[Xdn])
                        chk("g_ev2")
                        S.op("dve", lambda e: e.tensor_tensor(out=xsD[:].rearrange("p (h d) -> p h d", d=HP), in0=pt3, in1=bc8(D_bc), op=ALU.mult), reads=[ptn, "D_bc"], writes=[xsDn])
                        chk("g_ev3")
                        S.op("dve", lambda e: e.tensor_copy(out=Bt[:], in_=pt[:, 512:640]), reads=[ptn], writes=[Btn])
                        chk("g_tr")
                        pc, pcn = pf.next()
                        S.op("pe", lambda e: e.matmul(pc[:, 0:128], lhsT=xcT[:, 16 + g, tk], rhs=xcT[:, 20 + g, tk], start=True, stop=True), reads=["xcT"], writes=[pcn])
                        cbm, cbmn = cbmr.next()
                        S.op("dve", lambda e: e.tensor_tensor(out=cbm[:], in0=pc[:, 0:128], in1=Umask[:], op=ALU.mult), reads=[pcn, "Umask"], writes=[cbmn])
                        chk("g_cb")
                        Rg, Rgn = gR[g]
                        chk("g_R")
                        mT, mTn = mTr.next()
                        for hh in range(2):
                            pseg, psegn = pf.next()
                            for hl in range(2):
                                S.op("pe", lambda e, hh=hh, hl=hl: e.matmul(pseg[:], lhsT=Lmask_b[:], rhs=Rg[:, hl, hh * 4:(hh + 1) * 4, :], start=(hl == 0), stop=(hl == 1)),
                                     reads=["Lmask_b", Rgn], writes=[psegn])
                            Et, Etn = Er.next()
                            S.op("act", lambda e: e.activation(out=Et[:], in_=pseg[:], func=AF.Exp), reads=[psegn], writes=[Etn])
                            S.op("dve", lambda e, hh=hh: e.tensor_tensor(out=mT[:, hh * 4:(hh + 1) * 4, :], in0=Et[:].rearrange("p (h t) -> p h t", t=128),
                                                                         in1=cbm[:].unsqueeze(1).to_broadcast([128, 4, 128]), op=ALU.mult),
                                 reads=[Etn, cbmn], writes=[mTn])
                        gst[g] = (Xg, Xgn, Xd, Xdn, xsD, xsDn, Bt, Btn, mT, mTn)

                    def ssd_back(g):
                        Xg, Xgn, Xd, Xdn, xsD, xsDn, Bt, Btn, mT, mTn = gst[g]

                        def bc8(ap2):
                            return ap2[:, g * 8:(g + 1) * 8].unsqueeze(2).to_broadcast([128, 8, HP])
                        chk("g_seg")
                        pyo, pyon = pf.next()
                        S.op("pe", lambda e: e.matmul(pyo[:], lhsT=xcT[:, 20 + g, tk], rhs=STb[:, g * 512:(g + 1) * 512], start=True, stop=True),
                             reads=["xcT", "STb%d" % g], writes=[pyon])
                        pyd, pydn = pf.next()
                        S.op("pe", lambda e: e.matmul(pyd[:], lhsT=ident_b[:], rhs=xsD[:], start=True, stop=False), reads=["ident_b", xsDn], writes=[pydn])
                        for h8 in range(8):
                            S.op("pe", lambda e, h8=h8: e.matmul(pyd[:, h8 * HP:(h8 + 1) * HP], lhsT=mT[:, h8, :], rhs=Xg[:, h8 * HP:(h8 + 1) * HP], start=False, stop=True),
                                 reads=[mTn, Xgn], writes=[pydn])
                        yg_ = ybuf[:, g * 512:(g + 1) * 512]
                        S.op("dve", lambda e: e.tensor_tensor(out=yg_.rearrange("p (h d) -> p h d", d=HP), in0=pyo[:].rearrange("p (h d) -> p h d", d=HP),
                                                              in1=bc8(ea), op=ALU.mult), reads=[pyon, smn], writes=["ybuf%d" % g])
                        S.op("dve", lambda e: e.tensor_tensor(out=yg_, in0=pyd[:], in1=yg_, op=ALU.add), reads=[pydn, "ybuf%d" % g], writes=["ybuf%d" % g])
                        chk("g_y")
                        pcs, pcsn = pf.next()
                        S.op("pe", lambda e: e.matmul(pcs[:], lhsT=Bt[:], rhs=Xd[:], start=True, stop=True), reads=[Btn, Xdn], writes=[pcsn])
                        chk("g_cs")
                        STg = ST[:, g * 512:(g + 1) * 512]
                        S.op("pool", lambda e: e.tensor_tensor(out=STg.rearrange("p (h d) -> p h d", d=HP), in0=STg.rearrange("p (h d) -> p h d", d=HP),
                                                               in1=bc8(decb), op=ALU.mult), reads=["ST%d" % g, smn], writes=["ST%d" % g])
                        S.op("dve", lambda e: e.tensor_tensor(out=STg, in0=pcs[:], in1=STg, op=ALU.add), reads=[pcsn, "ST%d" % g], writes=["ST%d" % g])
                        S.op("act", lambda e: e.activation(out=STb[:, g * 512:(g + 1) * 512], in_=STg, func=AF.Copy), reads=["ST%d" % g], writes=["STb%d" % g])
                        chk("g_end")


                    def gate():
                        S.op("dve", lambda e: e.tensor_tensor(out=ygb[:], in0=ybuf[:], in1=zs[:, i, :], op=ALU.mult), reads=["ybuf0", "ybuf1", "ybuf2", "ybuf3"] + ["zs%d" % i], writes=["ygb"])
                        ss2, ss2n = ss2r.next()
                        S.op("act", lambda e: e.activation(out=junk[:], in_=ygb[:, 0:1024], func=AF.Square, accum_out=ss2[:, 0:1]), reads=["ygb"], writes=["junk", ss2n])
                        S.op("act", lambda e: e.activation(out=junk[:], in_=ygb[:, 1024:2048], func=AF.Square, accum_out=ss2[:, 1:2]), reads=["ygb"], writes=["junk", ss2n])
                        S.op("dve", lambda e: e.tensor_tensor(out=ss2[:, 0:1], in0=ss2[:, 0:1], in1=ss2[:, 1:2], op=ALU.add), reads=[ss2n], writes=[ss2n])
                        S.op("dve", lambda e: e.tensor_scalar(out=ss2[:, 1:2], in0=ss2[:, 0:1], scalar1=1.0 / DI, scalar2=EPS, op0=ALU.mult, op1=ALU.add), reads=[ss2n], writes=[ss2n])
                        S.op("act", lambda e: e.activation(out=ss2[:, 2:3], in_=ss2[:, 1:2], func=AF.Ln), reads=[ss2n], writes=[ss2n])
                        S.op("act", lambda e: e.activation(out=ss2[:, 3:4], in_=ss2[:, 2:3], func=AF.Exp, scale=-0.5), reads=[ss2n], writes=[ss2n])
                        wo[i] = (ss2, ss2n)
                        for hf in range(2):
                            pt, ptn = pb.next()
                            for k in range(8):
                                kk = hf * 8 + k
                                S.op("pe", lambda e, k=k, kk=kk: e.transpose(out=pt[:, k * 128:(k + 1) * 128], in_=ygb[:, kk * 128:(kk + 1) * 128], identity=ident_b[:]),
                                     reads=["ygb", "ident_b"], writes=[ptn])
                            S.op("dve", lambda e: e.tensor_tensor(out=ygT[:, hf * 8:(hf + 1) * 8, tk], in0=pt[:].rearrange("p (k t) -> p k t", t=128),
                                                                  in1=g_gate[:, hf * 8:(hf + 1) * 8].unsqueeze(2).to_broadcast([128, 8, 128]), op=ALU.mult),
                                 reads=[ptn, "g_gate"], writes=["ygT"])

                        if is_samp or c == final_state_at:
                            dst = ssm_s[i] if is_samp else ssm_p
                            for jj in range(4):
                                pst, pstn = pf.next()
                                for u in range(4):
                                    j2 = jj * 4 + u
                                    S.op("pe", lambda e, j2=j2, u=u: e.transpose(out=pst[:, u * 128:(u + 1) * 128], in_=ST[:, j2 * 128:(j2 + 1) * 128], identity=ident_f[:]),
                                         reads=["ST%d" % jj, "ident_f"], writes=[pstn])
                                S.op("dve", lambda e: e.tensor_copy(out=ybuf[:, jj * 512:(jj + 1) * 512], in_=pst[:]), reads=[pstn], writes=["ybuf%d" % jj])
                            S.dma("sp", dst.rearrange("(j q) n -> q j n", q=128), ybuf[:].rearrange("q (j n) -> q j n", n=128), reads=["ybuf0", "ybuf1", "ybuf2", "ybuf3"], writes=["ssm_out"])

                    return ssd_front, ssd_back, gate, state_load, ssd_R

                chunks = [make_chunk(i) for i in range(4)]
                seq = [(i, g) for i in range(4) for g in range(NG)]
                chunks[0][4](0)
                chunks[0][4](1)
                chunks[0][0](0)
                for k, (i, g) in enumerate(seq):
                    if k + 2 < len(seq):
                        chunks[seq[k + 2][0]][4](seq[k + 2][1])
                    if k + 1 < len(seq):
                        chunks[seq[k + 1][0]][0](seq[k + 1][1])
                    if g == 0 and is_samp:
                        chunks[i][3]()
                    chunks[i][1](g)
                    if g == NG - 1:
                        chunks[i][2]()
                chk("gate")
                for hf in range(2):
                    w0, w0n = w_get()
                    w1, w1n = w_get()
                    for i in range(4):
                        c = st * 4 + i
                        xtn = "x4_%d" % i
                        ps, psn = pf.next()
                        for kk in range(16):
                            wt, wtn = (w0, w0n) if kk < 8 else (w1, w1n)
                            S.op("pe", lambda e, kk=kk, wt=wt: e.matmul(ps[:], lhsT=ygT[:, kk, i * 128:(i + 1) * 128], rhs=wt[:, kk % 8, :], start=(kk == 0), stop=(kk == 15)),
                                 reads=["ygT", wtn], writes=[psn])
                        ss2, ss2n = wo[i]
                        hsl = x4[:, i, hf * 512:(hf + 1) * 512]
                        S.op("dve", lambda e: e.scalar_tensor_tensor(out=hsl, in0=ps[:], scalar=ss2[:, 3:4], in1=hsl, op0=ALU.mult, op1=ALU.add),
                             reads=[psn, ss2n, xtn], writes=[xtn])
                        if hf == 1:
                            S.dma("sp", hAscr[c * 128:(c + 1) * 128, :], x4[:, i, :], reads=[xtn], writes=["hAscr"])
                            if dbg and not is_samp and c < 16:
                                S.dma("sp", yp[c * 128:(c + 1) * 128, :], x4[:, i, :], reads=[xtn], writes=["yp"])
                            rms_T(x4[:, i, :], xtn, g_kv[:], "g_kv", fT[:, :, i * 128:(i + 1) * 128], "fT")
                chk("outproj")
                for j in range(4):
                    wt, wtn = w_get()
                    for i in range(4):
                        c = st * 4 + i
                        ps, psn = pf.next()
                        for k in range(8):
                            S.op("pe", lambda e, k=k: e.matmul(ps[:], lhsT=fT[:, k, i * 128:(i + 1) * 128], rhs=wt[:, k, :], start=(k == 0), stop=(k == 7)),
                                 reads=[wtn, "fT"], writes=[psn])
                        kvb, kvbn = kvbr.next()
                        S.op("act", lambda e: e.activation(out=kvb[:], in_=ps[:], func=AF.Copy), reads=[psn], writes=[kvbn])
                        if is_samp or c >= NCHP - 16:
                            kvt, kvtn = kvtr.next()
                            S.op("act", lambda e: e.activation(out=kvt[:], in_=ps[:], func=AF.Copy), reads=[psn], writes=[kvtn])
                            if is_samp:
                                S.dma("sp", kv_s[i * 4:(i + 1) * 4, j * 512:(j + 1) * 512], kvt[0:4, :], reads=[kvtn], writes=["kv_out"])
                            else:
                                r0 = (c - (NCHP - 16)) * 128
                                S.dma("sp", kv_p[r0:r0 + 128, j * 512:(j + 1) * 512], kvt[:], reads=[kvtn], writes=["kv_out"])
                        pt, ptn = pb.next()
                        for u in range(4):
                            S.op("pe", lambda e, u=u: e.transpose(out=pt[:, u * 128:(u + 1) * 128], in_=kvb[:, u * 128:(u + 1) * 128], identity=ident_b[:]),
                                 reads=[kvbn, "ident_b"], writes=[ptn])
                        half = i // 2
                        S.op("dve", lambda e: e.tensor_copy(out=kTst[:, j * 4:(j + 1) * 4, (i % 2) * 128:(i % 2 + 1) * 128],
                                                            in_=pt[:, 0:512].rearrange("p (u t) -> p u t", t=128)),
                             reads=[ptn], writes=["kTst%d_%d" % (j, i % 2)])
                        if i % 2 == 1:
                            t0 = st * 512 + half * 256
                            S.dma("sp", kvT[j * 4:(j + 1) * 4, :, t0:t0 + 256].rearrange("u p t -> p u t"), kTst[:, j * 4:(j + 1) * 4, :],
                                  reads=["kTst%d_0" % j, "kTst%d_1" % j], writes=["kvT"])

            S.barrier()
            es1.close()
            chk("p2")

            def sb3(name, shape, dt=F32):
                return sb(name, shape, dt, es=es3)

            SCL = 1.0 / math.sqrt(128.0)

            def sb3b(name, shape, dt=F32):
                return sb(name, shape, dt, es=es3b)

            gsil = sb3("gsil", [128, 2048], BF16)
            Btl = sb3("Btl", [128, 24, 256])
            ogT = sb3("ogT", [128, 8, 2048], BF16)
            hBr = Ring(es3, nc, "hB", [128, D], F32, 4)
            fn_bc = sb3("fn_bc", [128, D])
            ones_b = sb3("ones_b", [128, 128], BF16)
            stat = sb3("stat", [128, 32])
            sflag = sb3("sflag", [128, 2])
            hnT = sb3b("hnT", [128, 8, 2048], BF16)
            QT = sb3b("QT", [128, 3, 2048], BF16)
            KTw = sb3b("KTw", [128, 4096], BF16)
            VTw = sb3b("VTw", [128, 4096], BF16)
            acc = sb3b("acc", [128, 2, 2048])
            sq = sb3b("sq", [128, 2048], BF16)
            stg = sb3b("stg", [128, 2048], BF16)
            tmpr = Ring(es3b, nc, "tmpS", [128, 256], F32, 4)
            PTr = Ring(es3b, nc, "PT", [128, 256], BF16, 4)
            Vbr = Ring(es3b, nc, "Vb", [128, 128], BF16, 8)

            S.op("dve", lambda e: e.memset(ones_b[:], 1.0), writes=["ones_b"])
            with nc.allow_non_contiguous_dma(reason="small parameter loads"):
                S.dma("sp", fn_bc[:], final_norm.partition_broadcast(128), writes=["fn_bc"])
            S.dma("sp", sflag[:], spanflag, writes=["sflag"])
            with nc.allow_non_contiguous_dma(reason="Toeplitz bias tiles (one-time)"):
                for gh in range(24):
                    S.dma("pool", Btl[:, gh, 0:128], bass.AP(fbig_t, gh * 128 * 383 + 255, [[382, 128], [1, 128]]), reads=["fbig"], writes=["Btl"])
                    S.dma("pool", Btl[:, gh, 128:256], bass.AP(fbig_t, gh * 128 * 383 + 127, [[382, 128], [1, 128]]), reads=["fbig"], writes=["Btl"])

            def load_hA(ci):
                hA_, hAn_ = hBr.next()
                hB_, hBn_ = hBr.next()
                S.dma("sp", hA_[:], hAscr[ci * 128:(ci + 1) * 128, :], reads=["hAscr"], writes=[hAn_])
                S.dma("sp", hB_[:], hAscr[2048 + ci * 128:2048 + (ci + 1) * 128, :], reads=["hAscr"], writes=[hBn_])
                S.op("dve", lambda e: e.tensor_tensor(out=hB_[:], in0=hB_[:], in1=hA_[:], op=ALU.subtract), reads=[hAn_, hBn_], writes=[hBn_])
                S.op("dve", lambda e: e.scalar_tensor_tensor(out=hA_[:], in0=hB_[:], scalar=sflag[:, 0:1], in1=hA_[:], op0=ALU.mult, op1=ALU.add),
                     reads=[hAn_, hBn_, "sflag"], writes=[hAn_])
                return hA_, hAn_
            chk("p3c")

            def out_proj(T0, NT, is_samp):
                w0, w0n = w_get()
                w1, w1n = w_get()
                for ci in range(NT // 128):
                    if is_samp:
                        hB, hBn = hBr.next()
                        S.dma("sp", hB[:], hAscr[T0 + ci * 128:T0 + (ci + 1) * 128, :], reads=["hAscr"], writes=[hBn])
                    else:
                        hB, hBn = load_hA(ci)
                    for hf, (wt, wtn) in enumerate(((w0, w0n), (w1, w1n))):
                        ps, psn = pf.next()
                        for k in range(8):
                            S.op("pe", lambda e, k=k: e.matmul(ps[:], lhsT=ogT[:, k, ci * 128:(ci + 1) * 128], rhs=wt[:, k, :], start=(k == 0), stop=(k == 7)),
                                 reads=["ogT", wtn], writes=[psn])
                        S.op("dve", lambda e: e.tensor_tensor(out=hB[:, hf * 512:(hf + 1) * 512], in0=ps[:], in1=hB[:, hf * 512:(hf + 1) * 512], op=ALU.add), reads=[psn, hBn], writes=[hBn])
                    ss, ssn = ssr.next()
                    S.op("act", lambda e: e.activation(out=junk[:], in_=hB[:], func=AF.Square, accum_out=ss[:, 0:1]), reads=[hBn], writes=["junk", ssn])
                    S.op("dve", lambda e: e.tensor_scalar(out=ss[:, 1:2], in0=ss[:, 0:1], scalar1=1.0 / D, scalar2=EPS, op0=ALU.mult, op1=ALU.add), reads=[ssn], writes=[ssn])
                    S.op("act", lambda e: e.activation(out=ss[:, 2:3], in_=ss[:, 1:2], func=AF.Ln), reads=[ssn], writes=[ssn])
                    S.op("act", lambda e: e.activation(out=ss[:, 3:4], in_=ss[:, 2:3], func=AF.Exp, scale=-0.5), reads=[ssn], writes=[ssn])
                    S.op("dve", lambda e: e.scalar_tensor_tensor(out=hB[:], in0=hB[:], scalar=ss[:, 3:4], in1=fn_bc[:], op0=ALU.mult, op1=ALU.mult), reads=[hBn, ssn, "fn_bc"], writes=[hBn])
                    if is_samp:
                        S.dma("sp", ysm[ci * 4:(ci + 1) * 4, :], hB[0:4, :], reads=[hBn], writes=["y_out"])
                    else:
                        S.dma("sp", yp[ci * 128:(ci + 1) * 128, :], hB[:], reads=[hBn], writes=["y_out%d" % (ci % 2)])


            for sp in ([1] if (0 in spans or 1 in spans) else []):
                is_samp = False
                T0 = 0
                NT = 2048
                W = 4096
                for ci in range(NT // 128):
                    hB, hBn = load_hA(ci)
                    rms_T(hB[:], hBn, g_b[:], "g_b", hnT[:, :, ci * 128:(ci + 1) * 128], "hnT")
                chk("p3a")
                for h in range(8):
                    wt, wtn = w_get()
                    for (Win, Wn, hidx) in ((KTw, "KTw", h), (VTw, "VTw", 8 + h)):
                        S.dma("sp", sq[:], kvT[hidx, :, 0:2048], reads=["kvT"], writes=["sq"])
                        S.dma("sp", stg[:], kvT[hidx, :, 2048:4096], reads=["kvT"], writes=["stg"])
                        S.op("act", lambda e, Win=Win: e.activation(out=Win[:, 0:2048], in_=sq[:], func=AF.Copy, scale=sflag[:, 0:1]), reads=["sq", "sflag"], writes=[Wn])
                        S.op("dve", lambda e: e.tensor_tensor(out=stg[:], in0=stg[:], in1=sq[:], op=ALU.subtract), reads=["sq", "stg"], writes=["stg"])
                        S.op("dve", lambda e, Win=Win: e.scalar_tensor_tensor(out=Win[:, 2048:4096], in0=stg[:], scalar=sflag[:, 0:1], in1=sq[:], op0=ALU.mult, op1=ALU.add),
                             reads=["sq", "stg", "sflag"], writes=[Wn])
                    for tt in range(NT // 512):
                        tsl = slice(tt * 512, (tt + 1) * 512)
                        for q in range(4):
                            ps, psn = pf.next()
                            for k in range(8):
                                S.op("pe", lambda e, k=k: e.matmul(ps[:], lhsT=wt[:, k, q * 128:(q + 1) * 128], rhs=hnT[:, k, tsl], start=(k == 0), stop=(k == 7)),
                                     reads=[wtn, "hnT"], writes=[psn])
                            if q < 3:
                                S.op("dve", lambda e: e.tensor_scalar(out=QT[:, q, tsl], in0=ps[:], scalar1=SCL, scalar2=None, op0=ALU.mult), reads=[psn], writes=["QT"])
                            else:
                                S.op("act", lambda e: e.activation(out=gsil[:, tsl], in_=ps[:], func=AF.Silu), reads=[psn], writes=["gsil"])
                    nst = 0
                    for src, n2 in [(QT[:, 0, :], NT), (QT[:, 1, :], NT), (QT[:, 2, :], NT), (KTw[:, 0:2048], 2048)] + ([(KTw[:, 2048:4096], 2048)] if W > 2048 else []):
                        rn = ["QT"] if nst < 3 else ["KTw"]
                        S.op("act", lambda e: e.activation(out=sq[:, 0:n2], in_=src, func=AF.Square), reads=rn, writes=["sq"])
                        for tt in range(n2 // 512):
                            ps, psn = pf.next()
                            S.op("pe", lambda e: e.matmul(ps[:], lhsT=ones_b[:], rhs=sq[:, tt * 512:(tt + 1) * 512], start=True, stop=True), reads=["ones_b", "sq"], writes=[psn])
                            col = (0 if nst < 3 else 16) + (nst % 3 if nst < 3 else nst - 3) * 4 + tt
                            S.op("dve", lambda e, col=col: e.reduce_max(out=stat[:, col:col + 1], in_=ps[:], axis=AX.X), reads=[psn], writes=["stat"])
                        nst += 1
                    nkc = 16 + 4 * (nst - 3)
                    S.op("dve", lambda e: e.reduce_max(out=stat[:, 30:31], in_=stat[:, 0:12], axis=AX.X), reads=["stat"], writes=["stat"])
                    S.op("dve", lambda e: e.reduce_max(out=stat[:, 31:32], in_=stat[:, 16:nkc], axis=AX.X), reads=["stat"], writes=["stat"])
                    S.op("dve", lambda e: e.tensor_tensor(out=stat[:, 30:31], in0=stat[:, 30:31], in1=stat[:, 31:32], op=ALU.mult), reads=["stat"], writes=["stat"])
                    S.op("act", lambda e: e.activation(out=stat[:, 30:31], in_=stat[:, 30:31], func=AF.Ln), reads=["stat"], writes=["stat"])
                    S.op("act", lambda e: e.activation(out=stat[:, 30:31], in_=stat[:, 30:31], func=AF.Exp, scale=0.5), reads=["stat"], writes=["stat"])
                    S.op("dve", lambda e: e.scalar_tensor_tensor(out=stat[:, 29:30], in0=stat[:, 30:31], scalar=-1.02, in1=bmax[:, 0:1], op0=ALU.mult, op1=ALU.subtract),
                         reads=["stat", "bmax"], writes=["stat"])
                    S.op("dve", lambda e: e.tensor_scalar(out=stat[:, 29:30], in0=stat[:, 29:30], scalar1=-1.0, scalar2=None, op0=ALU.add), reads=["stat"], writes=["stat"])
                    negM = stat[:, 29:30]
                    S.op("pool", lambda e: e.memset(acc[:], 0.0), writes=["acc"])
                    tiles = []
                    for g, d in enumerate((1, 4, 16)):
                        for r in range(d):
                            for nb in range((2048 // d) // 128):
                                tiles.append((g, d, r, nb))
                    tst = {}

                    def stageA(i):
                        g, d, r, nb = tiles[i]
                        gh = g * 8 + h
                        nblk = (2048 // d) // 128
                        Qv = QT[:, g, :].rearrange("p (j d) -> p j d", d=d)
                        Kv = KTw[:].rearrange("p (j d) -> p j d", d=d)
                        Vv = VTw[:].rearrange("p (j d) -> p j d", d=d)
                        n = nblk + nb
                        has_prev = True
                        lo = 0
                        qa = Qv[:, nb * 128:(nb + 1) * 128, r]
                        ps, psn = pf.next()
                        if has_prev:
                            S.op("pe", lambda e: e.matmul(ps[:, 0:128], lhsT=Kv[:, (n - 1) * 128:n * 128, r], rhs=qa, start=True, stop=True), reads=["KTw", "QT"], writes=[psn])
                        S.op("pe", lambda e: e.matmul(ps[:, 128:256], lhsT=Kv[:, n * 128:(n + 1) * 128, r], rhs=qa, start=True, stop=True), reads=["KTw", "QT"], writes=[psn])
                        tmp, tmpn = tmpr.next()
                        if nb == 0:
                            S.op("dve", lambda e: e.scalar_tensor_tensor(out=tmp[:, 0:128], in0=ps[:, 0:128], scalar=sflag[:, 1:2], in1=Btl[:, gh, 0:128], op0=ALU.add, op1=ALU.add),
                                 reads=[psn, "Btl", "sflag"], writes=[tmpn])
                            S.op("dve", lambda e: e.tensor_tensor(out=tmp[:, 128:256], in0=ps[:, 128:256], in1=Btl[:, gh, 128:256], op=ALU.add), reads=[psn, "Btl"], writes=[tmpn])
                        else:
                            S.op("dve", lambda e: e.tensor_tensor(out=tmp[:, lo:256], in0=ps[:, lo:256], in1=Btl[:, gh, lo:256], op=ALU.add), reads=[psn, "Btl"], writes=[tmpn])
                        PT, PTn = PTr.next()
                        S.op("act", lambda e: e.activation(out=PT[:, lo:256], in_=tmp[:, lo:256], func=AF.Exp, bias=negM), reads=[tmpn, "stat"], writes=[PTn])

                        def vblock(nk):
                            pt, ptn = pb.next()
                            S.op("pe", lambda e: e.transpose(out=pt[:, 0:128], in_=Vv[:, nk * 128:(nk + 1) * 128, r], identity=ident_b[:]), reads=["VTw", "ident_b"], writes=[ptn])
                            vb, vbn = Vbr.next()
                            S.op("dve", lambda e: e.tensor_copy(out=vb[:], in_=pt[:, 0:128]), reads=[ptn], writes=[vbn])
                            return vb, vbn
                        vprev = None
                        if has_prev:
                            vprev = vblock(n - 1) if nb == 0 else tst[i - 1]["vcur"]
                        vcur = vblock(n)
                        tst[i] = dict(PT=PT, PTn=PTn, vprev=vprev, vcur=vcur, has_prev=has_prev)

                    def stageB(i):
                        g, d, r, nb = tiles[i]
                        t_ = tst[i]
                        PT, PTn, vprev, vcur, has_prev = t_["PT"], t_["PTn"], t_["vprev"], t_["vcur"], t_["has_prev"]
                        Av = acc[:].rearrange("p a (j d) -> p a j d", d=d)
                        po, pon = pf.next()
                        if has_prev:
                            S.op("pe", lambda e: e.matmul(po[:, 0:128], lhsT=vprev[0][:], rhs=PT[:, 0:128], start=True, stop=False), reads=[vprev[1], PTn], writes=[pon])
                        S.op("pe", lambda e: e.matmul(po[:, 0:128], lhsT=vcur[0][:], rhs=PT[:, 128:256], start=not has_prev, stop=True), reads=[vcur[1], PTn], writes=[pon])
                        if has_prev:
                            S.op("pe", lambda e: e.matmul(po[:, 128:256], lhsT=ones_b[:], rhs=PT[:, 0:128], start=True, stop=False), reads=["ones_b", PTn], writes=[pon])
                        S.op("pe", lambda e: e.matmul(po[:, 128:256], lhsT=ones_b[:], rhs=PT[:, 128:256], start=not has_prev, stop=True), reads=["ones_b", PTn], writes=[pon])
                        av = Av[:, :, nb * 128:(nb + 1) * 128, r]
                        S.op("dve", lambda e: e.tensor_tensor(out=av, in0=po[:, 0:256].rearrange("p (a q) -> p a q", a=2), in1=av, op=ALU.add), reads=[pon, "acc"], writes=["acc"])
                        if i >= 1:
                            tst.pop(i - 1, None)

                    APIPE = 2
                    for i in range(min(APIPE, len(tiles))):
                        stageA(i)
                    for i in range(len(tiles)):
                        if i + APIPE < len(tiles):
                            stageA(i + APIPE)
                        stageB(i)
                    S.op("dve", lambda e: e.reciprocal(out=acc[:, 1, :], in_=acc[:, 1, :]), reads=["acc"], writes=["acc"])
                    S.op("dve", lambda e: e.tensor_tensor(out=acc[:, 0, :], in0=acc[:, 0, :], in1=acc[:, 1, :], op=ALU.mult), reads=["acc"], writes=["acc"])
                    S.op("dve", lambda e: e.tensor_tensor(out=ogT[:, h, 0:NT], in0=acc[:, 0, :], in1=gsil[:, 0:NT], op=ALU.mult), reads=["acc", "gsil"], writes=["ogT"])
                    chk("p3h")
                out_proj(T0, NT, False)
                chk("p3s")
            S.barrier()
            es3b.close()
            if 2 in spans:
                es3c = es3

                def sbc(name, shape, dt=F32):
                    return sb(name, shape, dt, es=es3c)

                T0s = 4096
                hnTs = sbc("hnTs", [128, 8, 512], BF16)
                QTs = sbc("QTs", [128, 24, 16], BF16)
                gsils = sbc("gsils", [128, 8, 16])
                KTn = sbc("KTn", [128, 8, 16], BF16)
                KTs = sbc("KTs", [128, 4, 8, 128], BF16)
                Vs = sbc("Vs", [128, 4, 1024], BF16)
                Vnb = sbc("Vnb", [4, 1024], BF16)
                Og = sbc("Og", [4, 3, 8, 128])
                mlt = sbc("mlt", [4, 8, 24])
                browrep = sbc("browrep", [4, 24, 128])
                BsA = sbc("BsA", [4, 8, 128])
                Bn = sbc("Bn", [4, 24, 4])
                negoff = sbc("negoff", [4, 4])
                tSr = Ring(es3c, nc, "tS", [4, 516], F32, 4)
                eSr = Ring(es3c, nc, "eS", [4, 516], BF16, 4)
                eTr = Ring(es3c, nc, "eT", [128, 20], BF16, 3)
                osb = sbc("osb", [4, 8, 128])

                S.op("dve", lambda e: e.tensor_scalar(out=negoff[:], in0=ident_f[0:4, 0:4], scalar1=-NEG, scalar2=NEG, op0=ALU.mult, op1=ALU.add),
                     reads=["ident_f"], writes=["negoff"])
                for gh in range(24):
                    pt_, ptn_ = pf.next()
                    if gh < 8:
                        S.op("pe", lambda e: e.transpose(out=pt_[0:4, 0:128], in_=Btl[:, gh, 0:4], identity=ident_f[:]), reads=["Btl", "ident_f"], writes=[ptn_])
                        S.op("dve", lambda e: e.tensor_copy(out=BsA[:, gh, :], in_=pt_[0:4, 0:128]), reads=[ptn_], writes=["BsA"])
                        S.op("pe", lambda e: e.transpose(out=pt_[0:4, 128:132], in_=Btl[0:4, gh, 128:132], identity=ident_f[0:4, 0:4]), reads=["Btl", "ident_f"], writes=[ptn_])
                        S.op("dve", lambda e: e.tensor_copy(out=Bn[:, gh, :], in_=pt_[0:4, 128:132]), reads=[ptn_], writes=["Bn"])
                    else:
                        S.op("pe", lambda e: e.transpose(out=pt_[0:4, 0:128], in_=Btl[:, gh, 0:1].to_broadcast([128, 4]), identity=ident_f[:]), reads=["Btl", "ident_f"], writes=[ptn_])
                        S.op("dve", lambda e: e.tensor_copy(out=browrep[:, gh, :], in_=pt_[0:4, 0:128]), reads=[ptn_], writes=["browrep"])
                        S.op("pe", lambda e: e.transpose(out=pt_[0:4, 128:132], in_=Btl[0:4, gh, 128:129].to_broadcast([4, 4]), identity=ident_f[0:4, 0:4]), reads=["Btl", "ident_f"], writes=[ptn_])
                        S.op("dve", lambda e: e.tensor_tensor(out=Bn[:, gh, :], in0=pt_[0:4, 128:132], in1=ident_f[0:4, 0:4], op=ALU.mult), reads=[ptn_, "ident_f"], writes=["Bn"])
                        S.op("dve", lambda e: e.tensor_tensor(out=Bn[:, gh, :], in0=Bn[:, gh, :], in1=negoff[:], op=ALU.add), reads=["Bn", "negoff"], writes=["Bn"])
                chk("s_bias")
                for ci in range(4):
                    hB, hBn = hBr.next()
                    S.dma("sp", hB[:], hAscr[T0s + ci * 128:T0s + (ci + 1) * 128, :], reads=["hAscr"], writes=[hBn])
                    rms_T(hB[:], hBn, g_b[:], "g_b", hnTs[:, :, ci * 128:(ci + 1) * 128], "hnTs")
                with nc.allow_non_contiguous_dma(reason="new K^T columns (tiny)"):
                    for h in range(8):
                        S.dma("sp", KTn[:, h, :].rearrange("p (b t) -> p b t", t=4),
                              kvT[h, :, T0s:T0s + 512].rearrange("p (b t) -> p b t", t=128)[:, :, 0:4], reads=["kvT"], writes=["KTn"])
                for h in range(8):
                    wt, wtn = w_get()
                    ps, psn = pf.next()
                    for q in range(4):
                        for k in range(8):
                            S.op("pe", lambda e, k=k, q=q: e.matmul(ps[:, q * 16:(q + 1) * 16], lhsT=wt[:, k, q * 128:(q + 1) * 128],
                                                                    rhs=hnTs[:, k, :].rearrange("p (b t) -> p b t", t=128)[:, :, 0:4], start=(k == 0), stop=(k == 7)),
                                 reads=[wtn, "hnTs"], writes=[psn])
                    for q in range(3):
                        S.op("dve", lambda e, q=q: e.tensor_scalar(out=QTs[:, q * 8 + h, :], in0=ps[:, q * 16:(q + 1) * 16], scalar1=SCL, scalar2=None, op0=ALU.mult), reads=[psn], writes=["QTs"])
                    S.op("act", lambda e: e.activation(out=gsils[:, h, :], in_=ps[:, 48:64], func=AF.Silu), reads=[psn], writes=["gsils"])
                chk("s_q")
                for b in range(NSB):
                    hBv, hBvn = hBr.next()
                    S.dma("sp", hBv[0:4, :], kv_s[b * 4:(b + 1) * 4, 1024:2048], reads=["kv_out"], writes=[hBvn])
                    S.op("dve", lambda e: e.tensor_copy(out=Vnb[:], in_=hBv[0:4, :]), reads=[hBvn], writes=["Vnb"])
                    for g, d in enumerate((1, 4, 16)):
                        nblk = 1 if g == 0 else 4
                        ckb = ckv[b].rearrange("(j d) c -> j d c", d=d)
                        for t in range(nblk):
                            j0 = 1920 if g == 0 else (384 if g == 1 else 0)
                            rsel = 0 if g == 0 else t
                            Kf, Kfn = hBr.next()
                            S.dma("sp", Kf[:], ckb[j0:j0 + 128, rsel, 0:1024], writes=[Kfn])
                            Kb, Kbn = xsbr.next()
                            S.op("dve", lambda e: e.tensor_copy(out=Kb[:], in_=Kf[:]), reads=[Kfn], writes=[Kbn])
                            ptk, ptkn = pb.next()
                            for hh in range(8):
                                S.op("pe", lambda e, hh=hh: e.transpose(out=ptk[:, hh * 128:(hh + 1) * 128], in_=Kb[:, hh * 128:(hh + 1) * 128], identity=ident_b[:]),
                                     reads=[Kbn, "ident_b"], writes=[ptkn])
                            S.op("dve", lambda e: e.tensor_copy(out=KTs[:, t, :, :], in_=ptk[:].rearrange("p (h k) -> p h k", k=128)), reads=[ptkn], writes=["KTs"])
                            Vf, Vfn = hBr.next()
                            S.dma("sp", Vf[:], ckb[j0:j0 + 128, rsel, 1024:2048], writes=[Vfn])
                            S.op("pool", lambda e: e.tensor_copy(out=Vs[:, t, :], in_=Vf[:]), reads=[Vfn], writes=["Vs"])
                        sst = {}

                        def sA(h):
                            gh = g * 8 + h
                            ncol = nblk * 128
                            qa = QTs[:, gh, b * 4:(b + 1) * 4]
                            ps, psn = pf.next()
                            for t in range(nblk):
                                S.op("pe", lambda e, t=t: e.matmul(ps[0:4, t * 128:(t + 1) * 128], lhsT=qa, rhs=KTs[:, t, h, :], start=True, stop=True), reads=["QTs", "KTs"], writes=[psn])
                            ps2, ps2n = pf.next()
                            S.op("pe", lambda e: e.matmul(ps2[0:4, 0:4], lhsT=qa, rhs=KTn[:, h, b * 4:(b + 1) * 4], start=True, stop=True), reads=["QTs", "KTn"], writes=[ps2n])
                            tS, tSn = tSr.next()
                            if g == 0:
                                S.op("dve", lambda e: e.tensor_tensor(out=tS[:, 0:128], in0=ps[0:4, 0:128], in1=BsA[:, h, :], op=ALU.add), reads=[psn, "BsA"], writes=[tSn])
                            else:
                                S.op("dve", lambda e: e.tensor_tensor(out=tS[:, 0:512].rearrange("p (t k) -> p t k", k=128), in0=ps[0:4, 0:512].rearrange("p (t k) -> p t k", k=128),
                                                                      in1=browrep[:, gh, :].unsqueeze(1).to_broadcast([4, 4, 128]), op=ALU.add), reads=[psn, "browrep"], writes=[tSn])
                                S.op("dve", lambda e: e.tensor_tensor(out=tS[:, 0:512].rearrange("p (t k) -> p t k", k=128), in0=tS[:, 0:512].rearrange("p (t k) -> p t k", k=128),
                                                                      in1=negoff[:].unsqueeze(2).to_broadcast([4, 4, 128]), op=ALU.add), reads=[tSn, "negoff"], writes=[tSn])
                            S.op("dve", lambda e: e.tensor_tensor(out=tS[:, ncol:ncol + 4], in0=ps2[0:4, 0:4], in1=Bn[:, gh, :], op=ALU.add), reads=[ps2n, "Bn"], writes=[tSn])
                            S.op("dve", lambda e: e.reduce_max(out=mlt[:, 0, gh:gh + 1], in_=tS[:, 0:ncol + 4], axis=AX.X), reads=[tSn], writes=["mlt"])
                            S.op("dve", lambda e: e.tensor_scalar(out=mlt[:, 3, gh:gh + 1], in0=mlt[:, 0, gh:gh + 1], scalar1=-1.0, scalar2=None, op0=ALU.mult), reads=["mlt"], writes=["mlt"])
                            eS, eSn = eSr.next()
                            S.op("act", lambda e: e.activation(out=eS[:, 0:ncol + 4], in_=tS[:, 0:ncol + 4], func=AF.Exp, bias=mlt[:, 3, gh:gh + 1], accum_out=mlt[:, 1, gh:gh + 1]),
                                 reads=[tSn, "mlt"], writes=[eSn, "mlt"])
                            sst[h] = (eS, eSn, ncol, gh)

                        def sB(h):
                            eS, eSn, ncol, gh = sst.pop(h)
                            pte, pten = pb.next()
                            for t in range(nblk):
                                S.op("pe", lambda e, t=t: e.transpose(out=pte[:, t * 4:(t + 1) * 4], in_=eS[:, t * 128:(t + 1) * 128], identity=ident_b[0:4, 0:4]), reads=[eSn, "ident_b"], writes=[pten])
                            S.op("pe", lambda e: e.transpose(out=pte[0:4, 16:20], in_=eS[:, ncol:ncol + 4], identity=ident_b[0:4, 0:4]), reads=[eSn, "ident_b"], writes=[pten])
                            eT, eTn = eTr.next()
                            S.op("dve", lambda e: e.tensor_copy(out=eT[:, 0:4 * nblk], in_=pte[:, 0:4 * nblk]), reads=[pten], writes=[eTn])
                            S.op("dve", lambda e: e.tensor_copy(out=eT[0:4, 16:20], in_=pte[0:4, 16:20]), reads=[pten, eTn], writes=[eTn])
                            po, pon = pf.next()
                            for t in range(nblk):
                                S.op("pe", lambda e, t=t: e.matmul(po[0:4, 0:128], lhsT=eT[:, t * 4:(t + 1) * 4], rhs=Vs[:, t, h * 128:(h + 1) * 128], start=(t == 0), stop=False), reads=[eTn, "Vs"], writes=[pon])
                            S.op("pe", lambda e: e.matmul(po[0:4, 0:128], lhsT=eT[0:4, 16:20], rhs=Vnb[:, h * 128:(h + 1) * 128], start=False, stop=True), reads=[eTn, "Vnb"], writes=[pon])
                            S.op("dve", lambda e: e.tensor_copy(out=Og[:, g, h, :], in_=po[0:4, 0:128]), reads=[pon], writes=["Og"])

                        sA(0)
                        sA(1)
                        for h in range(8):
                            if h + 2 < 8:
                                sA(h + 2)
                            sB(h)
                    chk("s_g")
                    m3 = mlt[:, 0, :].rearrange("p (g h) -> p g h", h=8)
                    l3 = mlt[:, 1, :].rearrange("p (g h) -> p g h", h=8)
                    w3 = mlt[:, 2, :].rearrange("p (g h) -> p g h", h=8)
                    Mx = mlt[:, 4, 0:8]
                    den = mlt[:, 5, 0:8]
                    S.op("dve", lambda e: e.tensor_tensor(out=Mx, in0=m3[:, 0, :], in1=m3[:, 1, :], op=ALU.max), reads=["mlt"], writes=["mlt"])
                    S.op("dve", lambda e: e.tensor_tensor(out=Mx, in0=Mx, in1=m3[:, 2, :], op=ALU.max), reads=["mlt"], writes=["mlt"])
                    S.op("dve", lambda e: e.tensor_tensor(out=w3, in0=m3, in1=Mx.unsqueeze(1).to_broadcast([4, 3, 8]), op=ALU.subtract), reads=["mlt"], writes=["mlt"])
                    S.op("act", lambda e: e.activation(out=mlt[:, 2, :], in_=mlt[:, 2, :], func=AF.Exp), reads=["mlt"], writes=["mlt"])
                    S.op("dve", lambda e: e.tensor_tensor(out=mlt[:, 3, :], in0=mlt[:, 2, :], in1=mlt[:, 1, :], op=ALU.mult), reads=["mlt"], writes=["mlt"])
                    S.op("dve", lambda e: e.tensor_tensor(out=den, in0=mlt[:, 3, 0:8], in1=mlt[:, 3, 8:16], op=ALU.add), reads=["mlt"], writes=["mlt"])
                    S.op("dve", lambda e: e.tensor_tensor(out=den, in0=den, in1=mlt[:, 3, 16:24], op=ALU.add), reads=["mlt"], writes=["mlt"])
                    S.op("dve", lambda e: e.reciprocal(out=den, in_=den), reads=["mlt"], writes=["mlt"])
                    S.op("dve", lambda e: e.tensor_tensor(out=w3, in0=w3, in1=den.unsqueeze(1).to_broadcast([4, 3, 8]), op=ALU.mult), reads=["mlt"], writes=["mlt"])
                    for g in range(3):
                        S.op("dve", lambda e, g=g: e.tensor_tensor(out=Og[:, g, :, :], in0=Og[:, g, :, :], in1=mlt[:, 2, g * 8:(g + 1) * 8].unsqueeze(2).to_broadcast([4, 8, 128]), op=ALU.mult),
                             reads=["Og", "mlt"], writes=["Og"])
                    S.op("dve", lambda e: e.tensor_tensor(out=osb[:], in0=Og[:, 0, :, :], in1=Og[:, 1, :, :], op=ALU.add), reads=["Og"], writes=["osb"])
                    S.op("dve", lambda e: e.tensor_tensor(out=osb[:], in0=osb[:], in1=Og[:, 2, :, :], op=ALU.add), reads=["Og", "osb"], writes=["osb"])
                    pso, pson = pf.next()
                    for h in range(8):
                        S.op("pe", lambda e, h=h: e.transpose(out=pso[:, h * 4:(h + 1) * 4], in_=osb[:, h, :], identity=ident_f[0:4, 0:4]), reads=["osb", "ident_f"], writes=[pson])
                    S.op("dve", lambda e: e.tensor_tensor(out=ogT[:, :, b * 128:b * 128 + 4], in0=pso[:, 0:32].rearrange("p (h t) -> p h t", t=4),
                                                          in1=gsils[:, :, b * 4:(b + 1) * 4], op=ALU.mult), reads=[pson, "gsils"], writes=["ogT"])
                chk("s_att")
                out_proj(T0s, 512, True)
        except _Stop:
            pass
        es3b.close()
        es3.close()
        es1.close()

        S.finish("sp")
    return nc


_PROG = {}


def _get_prog():
    if "nc" not in _PROG:
        _PROG["nc"] = build_program()
    return _PROG["nc"]


def _t5_bucket_np(dist):
    n = np.maximum(dist, 0)
    nf = np.maximum(n, 1).astype(np.float32)
    large = 16 + (np.log(nf / np.float32(16)) / np.float32(math.log(2048 / 16)) * np.float32(16)).astype(np.int32)
    large = np.minimum(large, 31)
    return np.where(n < 16, n, large)


def _onehot_const():
    oh = np.zeros((32, 3, 129), np.float32)
    for g, d in enumerate((1, 4, 16)):
        b = _t5_bucket_np(np.arange(129, dtype=np.int32) * d)
        oh[b, g, np.arange(129)] = 1.0
    return oh.reshape(32, 3 * 129)


def kernel(x_prompt, x_sample, state_ssm, state_conv, cache_kv, a_norm, a_w_in, a_conv_w, a_conv_b, a_dt_bias,
           a_A_log, a_D, a_gate_norm, a_w_out, rel_bias, kv_norm, w_kv, b_norm, b_w_in, b_w_out, final_norm):
    f = lambda a: np.ascontiguousarray(np.asarray(a, dtype=np.float32))
    shared = {
        "a_norm": f(a_norm[0]), "a_w_in": f(a_w_in[0]), "a_conv_w": f(a_conv_w[0]), "a_conv_b": f(a_conv_b[0]),
        "a_dt_bias": f(a_dt_bias[0]), "a_A_log": f(a_A_log[0]), "a_D": f(a_D[0]), "a_gate_norm": f(a_gate_norm[0]),
        "a_w_out": f(a_w_out[0]), "rel_bias": f(rel_bias), "kv_norm": f(kv_norm), "w_kv": f(w_kv),
        "b_norm": f(b_norm[0]), "b_w_in": f(b_w_in[0]), "b_w_out": f(b_w_out[0]), "final_norm": f(final_norm),
        "onehot": _onehot_const(),
    }
    in_maps = []
    for c in range(N_CORES):
        sl = slice(NSB * c, NSB * (c + 1))
        m = dict(shared)
        m["xp"] = f(x_prompt[c % 4])
        m["xs"] = f(np.asarray(x_sample)[sl].reshape(NSB * 4, D))
        m["sssm"] = f(np.asarray(state_ssm)[0, sl].reshape(NSB, DI, NS))
        m["sconv"] = f(np.asarray(state_conv)[0, sl])
        m["ckv"] = f(np.asarray(cache_kv)[sl].reshape(NSB, 2048, 2048))
        fl = np.zeros((128, 2), np.float32)
        fl[:, 0] = 1.0 if c >= 4 else 0.0
        fl[:, 1] = 0.0 if c >= 4 else -30000.0
        m["spanflag"] = fl
        in_maps.append(m)
    nc = _get_prog()
    res = run_bass_kernel_spmd(nc, in_maps, core_ids=list(range(N_CORES)))
    R = res.results
    y_prompt = np.stack([np.concatenate([R[b]["yp"], R[b + 4]["yp"]], 0) for b in range(4)], 0)
    y_sample = np.concatenate([R[c]["ys"].reshape(NSB, 4, D) for c in range(N_CORES)], 0)
    ssm_prompt = np.stack([R[b]["ssm_p"].reshape(NH, HP, NS) for b in range(4)], 0)[None]
    ssm_sample = np.concatenate([R[c]["ssm_s"].reshape(NSB, NH, HP, NS) for c in range(N_CORES)], 0)[None]
    conv_prompt = np.stack([R[b]["conv_p"] for b in range(4)], 0)[None]
    conv_sample = np.concatenate([R[c]["conv_s"] for c in range(N_CORES)], 0)[None]
    kv_prompt = np.stack([R[b]["kv_p"].reshape(2048, 2, 8, 128) for b in range(4)], 0)
    kv_sample = np.concatenate([R[c]["kv_s"].reshape(NSB, 4, 2, 8, 128) for c in range(N_CORES)], 0)
    return (y_prompt, y_sample, ssm_prompt, ssm_sample, conv_prompt, conv_sample, kv_prompt, kv_sample)
```

```python
import math
import numpy as np
import concourse.bass as bass
import concourse.mybir as mybir
from concourse.bass_utils import run_bass_kernel_spmd
from contextlib import ExitStack

F32 = mybir.dt.float32
BF16 = mybir.dt.bfloat16
AF = mybir.ActivationFunctionType
ALU = mybir.AluOpType
AX = mybir.AxisListType

N_CORES = 8
D = 1024
SEQ = 4096
DI = 2048
CD = 3072
NH = 32
HP = 64
NS = 128
NG = 4
IN_DIM = 5152
NCHP = SEQ // 128
NSB = 4
NCH = NCHP + NSB
NST = NCH // 4
LTOT = NCH * 128
EPS = 1e-5
N_DMA_SEMS = 48
NWBLK = 28


class Sched:
    def __init__(self, nc, es):
        self.nc = nc
        self.engs = {"pe": nc.tensor, "act": nc.scalar, "dve": nc.vector, "pool": nc.gpsimd, "sp": nc.sync}
        self.sem = {}
        self.cnt = {}
        for k in self.engs:
            self.sem[k] = es.enter_context(nc.semaphore("sem_" + k))
            self.cnt[k] = 0
        self.dsem = [es.enter_context(nc.semaphore("dsem%d" % i)) for i in range(N_DMA_SEMS)]
        self.dcnt = [0] * N_DMA_SEMS
        self.dnext = 0
        self.seen = {k: {} for k in self.engs}
        self.last_w = {}
        self.readers = {}
        self.n_inst = 0

    def _semh(self, key):
        return self.sem[key] if isinstance(key, str) else self.dsem[key]

    def _wait(self, E, stamps):
        eng = self.engs[E]
        best = {}
        for (k, v) in stamps:
            if E == "pe" and k == "pe":
                continue
            if self.seen[E].get(k, 0) >= v:
                continue
            if best.get(k, 0) < v:
                best[k] = v
        for k, v in best.items():
            eng.wait_ge(self._semh(k), v)
            self.seen[E][k] = v

    def _deps(self, reads, writes, E=None):
        deps = []
        for r in reads:
            if r in self.last_w:
                deps.append(self.last_w[r])
            if r[:2] in ("pf", "pb"):
                deps.extend(st for st in self.readers.get(r, ()) if st[0] != E)
        for w in writes:
            if w in self.last_w:
                deps.append(self.last_w[w])
            deps.extend(self.readers.get(w, ()))
        return deps

    def _commit(self, stamp, reads, writes):
        for r in reads:
            self.readers.setdefault(r, []).append(stamp)
        for w in writes:
            self.last_w[w] = stamp
            self.readers[w] = []

    def op(self, E, fn, reads=(), writes=()):
        reads = [r for r in reads if r is not None]
        writes = [w for w in writes if w is not None]
        self._wait(E, self._deps(reads, writes, E))
        ins = fn(self.engs[E])
        self.cnt[E] += 1
        ins.then_inc(self.sem[E], 1)
        self._commit((E, self.cnt[E]), reads, writes)
        self.n_inst += 1
        return ins

    def dma(self, Q, out, in_, reads=(), writes=(), **kw):
        reads = [r for r in reads if r is not None]
        writes = [w for w in writes if w is not None]
        i = self.dnext
        self.dnext = (self.dnext + 1) % N_DMA_SEMS
        deps = self._deps(reads, writes)
        if self.dcnt[i] > 0:
            deps.append((i, self.dcnt[i]))
        self._wait(Q, deps)
        ins = self.engs[Q].dma_start(out=out, in_=in_, **kw)
        self.dcnt[i] += 16
        ins.then_inc(self.dsem[i], 16)
        self._commit((i, self.dcnt[i]), reads, writes)
        self.n_inst += 1
        return ins

    def all_stamps(self):
        st = [(k, self.cnt[k]) for k in self.engs if self.cnt[k] > 0]
        st += [(i, self.dcnt[i]) for i in range(N_DMA_SEMS) if self.dcnt[i] > 0]
        return st

    def barrier(self):
        st = self.all_stamps()
        for E in self.engs:
            self._wait(E, [s for s in st if s[0] != E])
        self.last_w = {}
        self.readers = {}

    def finish(self, E="sp"):
        self._wait(E, [s for s in self.all_stamps() if s[0] != E])


class Ring:
    def __init__(self, es, nc, name, shape, dt, n, psum=False):
        mk = nc.psum_tensor if psum else nc.sbuf_tensor
        self.tiles = [es.enter_context(mk("%s%d" % (name, i), list(shape), dt)) for i in range(n)]
        self.names = ["%s%d" % (name, i) for i in range(n)]
        self.i = 0

    def next(self):
        t, nm = self.tiles[self.i], self.names[self.i]
        self.i = (self.i + 1) % len(self.tiles)
        return t, nm


class _Stop(Exception):
    pass


def build_program(st_list=None, dbg=False, final_state_at=None, stop=None, spans=(0, 1, 2)):
    _cnt = {}

    def chk(tag):
        _cnt[tag] = _cnt.get(tag, 0) + 1
        if stop is not None and stop.split(":")[0] == tag and _cnt[tag] >= int((stop + ":1").split(":")[1]):
            raise _Stop()
    if st_list is None:
        st_list = list(range(NST))
    if final_state_at is None:
        final_state_at = NCHP - 1
    nc = bass.Bass("TRN2", target_bir_lowering=False)

    def din(name, shape):
        return nc.dram_tensor(name, list(shape), F32, kind="ExternalInput").ap()

    def dout(name, shape):
        return nc.dram_tensor(name, list(shape), F32, kind="ExternalOutput").ap()

    xp = din("xp", [SEQ, D])
    xsm = din("xs", [NSB * 4, D])
    sssm = din("sssm", [NSB, DI, NS])
    sconv = din("sconv", [NSB, 3, CD])
    ckv = din("ckv", [NSB, 2048, 2048])
    a_norm = din("a_norm", [D])
    a_w_in = din("a_w_in", [D, IN_DIM])
    a_conv_w = din("a_conv_w", [4, CD])
    a_conv_b = din("a_conv_b", [CD])
    a_dt_bias = din("a_dt_bias", [NH])
    a_A_log = din("a_A_log", [NH])
    a_D = din("a_D", [NH])
    a_gate_norm = din("a_gate_norm", [DI])
    a_w_out = din("a_w_out", [DI, D])
    rel_bias = din("rel_bias", [32, 24])
    kv_norm = din("kv_norm", [D])
    w_kv = din("w_kv", [D, 2048])
    b_norm = din("b_norm", [D])
    b_w_in = din("b_w_in", [D, 4096])
    b_w_out = din("b_w_out", [D, D])
    final_norm = din("final_norm", [D])

    yp = dout("yp", [2048, D])
    ysm = dout("ys", [NSB * 4, D])
    ssm_p = dout("ssm_p", [DI, NS])
    ssm_s = dout("ssm_s", [NSB, DI, NS])
    conv_p = dout("conv_p", [3, CD])
    conv_s = dout("conv_s", [NSB, 3, CD])
    kv_p = dout("kv_p", [2048, 2048])
    kv_s = dout("kv_s", [NSB * 4, 2048])

    wscr = nc.dram_tensor("wscr", [NWBLK, 128, 4096], BF16).ap()
    hAscr = nc.dram_tensor("hAscr", [LTOT, D], F32).ap()
    kvT = nc.dram_tensor("kvT", [16, 128, LTOT], BF16).ap()
    fbias_t = nc.dram_tensor("fbias", [24, 383], F32)
    fbias = fbias_t.ap()
    fbig_t = nc.dram_tensor("fbig", [24, 128, 383], F32)
    fbig = fbig_t.ap()
    onehot = din("onehot", [32, 3 * 129])
    spanflag = din("spanflag", [128, 2])

    with ExitStack() as es0:
        S = Sched(nc, es0)
        E0 = es0.enter_context

        def sb(name, shape, dt=F32, es=es0):
            return es.enter_context(nc.sbuf_tensor(name, list(shape), dt))

        iot = sb("iot", [128, 128])
        ident_f = sb("ident_f", [128, 128])
        ident_b = sb("ident_b", [128, 128], BF16)
        Umask = sb("Umask", [128, 128])
        Lmask = sb("Lmask", [128, 128])
        ones_f = sb("ones_f", [128, 128])
        Lmask_b = sb("Lmask_b", [128, 128], BF16)
        Umask_b = sb("Umask_b", [128, 128], BF16)
        tmask = sb("tmask", [128, 1])
        S.op("pool", lambda e: e.iota(iot[:], [[1, 128]], channel_multiplier=-1, allow_small_or_imprecise_dtypes=True), writes=["iot"])
        S.op("dve", lambda e: e.tensor_single_scalar(out=ident_f[:], in_=iot[:], scalar=0.0, op=ALU.is_equal), reads=["iot"], writes=["ident_f"])
        S.op("dve", lambda e: e.tensor_single_scalar(out=ident_b[:], in_=iot[:], scalar=0.0, op=ALU.is_equal), reads=["iot"], writes=["ident_b"])
        S.op("dve", lambda e: e.tensor_single_scalar(out=Umask[:], in_=iot[:], scalar=0.0, op=ALU.is_ge), reads=["iot"], writes=["Umask"])
        S.op("dve", lambda e: e.tensor_single_scalar(out=Lmask[:], in_=iot[:], scalar=0.0, op=ALU.is_lt), reads=["iot"], writes=["Lmask"])
        S.op("dve", lambda e: e.memset(ones_f[:], 1.0), writes=["ones_f"])
        S.op("dve", lambda e: e.tensor_single_scalar(out=Lmask_b[:], in_=iot[:], scalar=0.0, op=ALU.is_lt), reads=["iot"], writes=["Lmask_b"])
        S.op("dve", lambda e: e.tensor_single_scalar(out=Umask_b[:], in_=iot[:], scalar=0.0, op=ALU.is_ge), reads=["iot"], writes=["Umask_b"])
        S.op("dve", lambda e: e.tensor_single_scalar(out=tmask[:], in_=iot[:, 0:1], scalar=-3.5, op=ALU.is_gt), reads=["iot"], writes=["tmask"])

        g_a = sb("g_a", [128, 8])
        g_kv = sb("g_kv", [128, 8])
        g_b = sb("g_b", [128, 8])
        g_gate = sb("g_gate", [128, 16])
        cw = sb("cw", [128, 24, 4])
        cb_ = sb("cb_", [128, 24])
        dtb_bc = sb("dtb_bc", [128, NH])
        aneg_bc = sb("aneg_bc", [128, NH])
        D_bc = sb("D_bc", [128, NH])
        wdt = sb("wdt", [128, 8, NH], BF16)
        with nc.allow_non_contiguous_dma(reason="tiny one-time parameter layouts"):
            S.dma("sp", g_a[:], a_norm.rearrange("(k p) -> p k", p=128), writes=["g_a"])
            S.dma("sp", g_kv[:], kv_norm.rearrange("(k p) -> p k", p=128), writes=["g_kv"])
            S.dma("sp", g_b[:], b_norm.rearrange("(k p) -> p k", p=128), writes=["g_b"])
            S.dma("sp", g_gate[:], a_gate_norm.rearrange("(k p) -> p k", p=128), writes=["g_gate"])
            for k in range(4):
                S.dma("sp", cw[:, :, k], a_conv_w[k].rearrange("(c p) -> p c", p=128), writes=["cw"])
            S.dma("sp", cb_[:], a_conv_b.rearrange("(c p) -> p c", p=128), writes=["cb_"])
            S.dma("sp", dtb_bc[:], a_dt_bias.partition_broadcast(128), writes=["dtb_bc"])
            S.dma("sp", aneg_bc[:], a_A_log.partition_broadcast(128), writes=["aneg_bc"])
            S.dma("sp", D_bc[:], a_D.partition_broadcast(128), writes=["D_bc"])
        S.op("act", lambda e: e.activation(out=aneg_bc[:], in_=aneg_bc[:], func=AF.Exp), reads=["aneg_bc"], writes=["aneg_bc"])
        S.op("dve", lambda e: e.tensor_scalar(out=aneg_bc[:], in0=aneg_bc[:], scalar1=-1.0, scalar2=None, op0=ALU.mult), reads=["aneg_bc"], writes=["aneg_bc"])

        def wsrc(w, c0, r0=0):
            return w[r0:r0 + 1024, c0:c0 + 512].rearrange("(k p) c -> p k c", p=128)

        def wdst(blk):
            return wscr[blk].rearrange("p (k c) -> p k c", c=512)

        S.dma("pool", wdt[:], a_w_in[:, 5120:5152].rearrange("(k p) c -> p k c", p=128), writes=["wdt"])
        cvt = []
        for j in range(6):
            cvt.append([(wdst(j), wsrc(a_w_in, 2048 + 512 * j))])
        for j in range(4):
            cvt.append([(wdst(6 + j), wsrc(a_w_in, 512 * j))])
        for hf in range(2):
            for kp in range(2):
                cvt.append([(wdst(10 + 2 * hf + kp), wsrc(a_w_out, 512 * hf, 1024 * kp))])
        for j in range(4):
            cvt.append([(wdst(14 + j), wsrc(w_kv, 512 * j))])
        for h in range(8):
            parts = []
            for q in range(4):
                c0 = (q * 8 + h) * 128 if q < 3 else 3072 + h * 128
                parts.append((wdst(18 + h)[:, :, q * 128:(q + 1) * 128], b_w_in[:, c0:c0 + 128].rearrange("(k p) c -> p k c", p=128)))
            cvt.append(parts)
        for hf in range(2):
            cvt.append([(wdst(26 + hf), wsrc(b_w_out, 512 * hf))])
        cstate = {"n": 0}
        CVT_DEPTH = 2

        def cvt_upto(blk):
            while cstate["n"] <= min(blk, NWBLK - 1):
                k = cstate["n"]
                rd = ["cvt%d" % (k - CVT_DEPTH)] if k >= CVT_DEPTH else []
                for (dst_, src_) in cvt[k]:
                    S.dma("pool", dst_, src_, reads=rd, writes=["wscr%d" % k, "cvt%d" % k])
                cstate["n"] += 1

        cvt_upto(3)

        wring = [sb("wring%d" % i, [128, 8, 512], BF16) for i in range(3)]
        wseq = []
        for st in st_list:
            wseq += list(range(0, 18))
        for sp in range(2):
            wseq += list(range(18, 28))
        wstate = {"i": 0, "issued": 0}

        def w_issue(idx):
            blk = wseq[idx]
            slot = idx % 3
            cvt_upto(blk + 3)
            S.dma("sp", wring[slot][:], wdst(blk), reads=["wscr%d" % blk], writes=["wring%d" % slot])

        def w_get():
            i = wstate["i"]
            while wstate["issued"] <= min(i + 1, len(wseq) - 1):
                w_issue(wstate["issued"])
                wstate["issued"] += 1
            wstate["i"] += 1
            return wring[i % 3], "wring%d" % (i % 3)

        pf = Ring(es0, nc, "pf", [128, 512], F32, 5, psum=True)
        pb = Ring(es0, nc, "pb", [128, 1024], BF16, 3, psum=True)

        junk = sb("junk", [128, 1024], BF16)
        ssr = Ring(es0, nc, "ssr", [128, 4], F32, 4)
        xsbr = Ring(es0, nc, "xsb", [128, D], BF16, 2)
        es1 = ExitStack()
        es3 = ExitStack()
        es3b = ExitStack()
        NEG = -30000.0
        bmax = sb("bmax", [128, 2])
        oh_t = sb("oh_t", [32, 3 * 129], es=es1)
        rb_t = sb("rb_t", [32, 24], es=es1)
        rbm = sb("rbm", [32, 4], es=es1)
        Fg_t = sb("Fg_t", [8, 383], es=es1)
        rep_t = sb("rep_t", [128, 383], es=es1)
        S.dma("sp", oh_t[:], onehot, writes=["oh_t"])
        S.dma("sp", rb_t[:], rel_bias, writes=["rb_t"])
        S.op("dve", lambda e: e.reduce_max(out=rbm[:, 0:1], in_=rb_t[:], axis=AX.X), reads=["rb_t"], writes=["rbm"])
        pq, pqn = pf.next()
        S.op("pe", lambda e: e.matmul(pq[0:1, 0:32], lhsT=rbm[:, 0:1], rhs=ident_f[0:32, 0:32], start=True, stop=True), reads=["rbm", "ident_f"], writes=[pqn])
        S.op("dve", lambda e: e.reduce_max(out=rbm[0:1, 1:2], in_=pq[0:1, 0:32], axis=AX.X), reads=[pqn, "rbm"], writes=["rbm"])
        pq2, pq2n = pf.next()
        S.op("pe", lambda e: e.matmul(pq2[:, 0:1], lhsT=ones_f[0:1, :], rhs=rbm[0:1, 1:2], start=True, stop=True), reads=["ones_f", "rbm"], writes=[pq2n])
        S.op("dve", lambda e: e.tensor_copy(out=bmax[:, 0:1], in_=pq2[:, 0:1]), reads=[pq2n], writes=["bmax"])
        for g in range(3):
            S.op("dve", lambda e: e.memset(Fg_t[:], NEG), writes=["Fg_t"])
            psg, psgn = pf.next()
            S.op("pe", lambda e: e.matmul(psg[0:8, 0:129], lhsT=rb_t[:, g * 8:(g + 1) * 8], rhs=oh_t[:, g * 129:(g + 1) * 129], start=True, stop=True),
                 reads=["rb_t", "oh_t"], writes=[psgn])
            S.op("dve", lambda e: e.tensor_copy(out=Fg_t[:, 127:256], in_=psg[0:8, 0:129]), reads=[psgn, "Fg_t"], writes=["Fg_t"])
            S.dma("sp", fbias[g * 8:(g + 1) * 8, :], Fg_t[:], reads=["Fg_t"], writes=["fbias"])
        with nc.allow_non_contiguous_dma(reason="replicate bias vectors (one-time)"):
            for gh in range(24):
                S.dma("sp", fbig[gh], fbias[gh].partition_broadcast(128), reads=["fbias"], writes=["fbig"])
        try:
            def sb1(name, shape, dt=F32):
                return sb(name, shape, dt, es=es1)

            x4 = sb1("x4", [128, 4, D])
            fT = sb1("fT", [128, 8, 512], BF16)
            rawr = Ring(es1, nc, "raw", [128, 4, 131], F32, 3)
            accr = Ring(es1, nc, "acc", [128, 4, 128], F32, 2)
            tails = sb1("tails", [128, 24, 3])
            csamp = sb1("csamp", [128, 24, NSB, 3])
            sconvT = sb1("sconvT", [128, 24, NSB, 3])
            xcT = sb1("xcT", [128, 24, 512], BF16)
            zs = sb1("zs", [128, 4, DI], BF16)
            smr = Ring(es1, nc, "smr", [128, 12, NH], F32, 4)
            ahlr = Ring(es1, nc, "ahl", [128, 2, NH], BF16, 4)
            Rr = Ring(es1, nc, "Rr", [128, 2, 8, 128], BF16, 3)
            Er = Ring(es1, nc, "Er", [128, 512], F32, 2)
            cbmr = Ring(es1, nc, "cbm", [128, 128], F32, 2)
            mTr = Ring(es1, nc, "mT", [128, 8, 128], BF16, 2)
            Xr = Ring(es1, nc, "Xg", [128, 512], BF16, 2)
            Xdr = Ring(es1, nc, "Xdg", [128, 512], BF16, 2)
            xsDr = Ring(es1, nc, "xsD", [128, 512], BF16, 2)
            Btr = Ring(es1, nc, "Bt", [128, 128], BF16, 2)
            ST = sb1("ST", [128, DI])
            STb = sb1("STb", [128, DI], BF16)
            ybuf = sb1("ybuf", [128, DI])
            ygb = sb1("ygb", [128, DI], BF16)
            ygT = sb1("ygT", [128, 16, 512], BF16)
            ss2r = Ring(es1, nc, "ss2r", [128, 4], F32, 4)
            hkb = sb1("hkb", [128, D], BF16)
            kvtr = Ring(es1, nc, "kvt", [128, 512], F32, 2)
            kvbr = Ring(es1, nc, "kvb", [128, 512], BF16, 2)
            kTst = sb1("kTst", [128, 16, 256], BF16)

            chk("alloc")
            S.op("pool", lambda e: e.memset(tails[:], 0.0), writes=["tails"])
            S.op("pool", lambda e: e.memset(ST[:], 0.0), writes=["ST0", "ST1", "ST2", "ST3"])
            S.op("pool", lambda e: e.memset(STb[:], 0.0), writes=["STb0", "STb1", "STb2", "STb3"])
            with nc.allow_non_contiguous_dma(reason="conv state rows -> feature-major (tiny)"):
                for b in range(NSB):
                    for r in range(3):
                        S.dma("sp", sconvT[:, :, b, r], sconv[b, r].rearrange("(c p) -> p c", p=128), writes=["sconvT"])

            def rms_T(src, src_name, gcol, gname, dst3, dst_name):
                ss, ssn = ssr.next()
                S.op("act", lambda e: e.activation(out=junk[:], in_=src, func=AF.Square, accum_out=ss[:, 0:1]),
                     reads=[src_name], writes=["junk", ssn])
                S.op("dve", lambda e: e.tensor_scalar(out=ss[:, 1:2], in0=ss[:, 0:1], scalar1=1.0 / D, scalar2=EPS, op0=ALU.mult, op1=ALU.add),
                     reads=[ssn], writes=[ssn])
                S.op("act", lambda e: e.activation(out=ss[:, 2:3], in_=ss[:, 1:2], func=AF.Ln), reads=[ssn], writes=[ssn])
                S.op("act", lambda e: e.activation(out=ss[:, 3:4], in_=ss[:, 2:3], func=AF.Exp, scale=-0.5), reads=[ssn], writes=[ssn])
                xb, xbn = xsbr.next()
                S.op("act", lambda e: e.activation(out=xb[:], in_=src, func=AF.Copy, scale=ss[:, 3:4]),
                     reads=[src_name, ssn], writes=[xbn])
                pt, ptn = pb.next()
                for k in range(8):
                    S.op("pe", lambda e, k=k: e.transpose(out=pt[:, k * 128:(k + 1) * 128], in_=xb[:, k * 128:(k + 1) * 128], identity=ident_b[:]),
                         reads=[xbn, "ident_b"], writes=[ptn])
                S.op("dve", lambda e: e.tensor_tensor(out=dst3, in0=pt[:].rearrange("p (k t) -> p k t", t=128),
                                                      in1=gcol.unsqueeze(2).to_broadcast([128, 8, 128]), op=ALU.mult),
                     reads=[ptn, gname], writes=[dst_name])
                return ss, ssn

            last_prompt_st = max([x for x in st_list if x < NST - 1], default=None)
            for st in st_list:
                is_samp = st == NST - 1
                for i in range(4):
                    c = st * 4 + i
                    xtn = "x4_%d" % i
                    if is_samp:
                        S.op("pool", lambda e: e.memset(x4[:, i, :], 0.0), writes=[xtn])
                        S.dma("sp", x4[0:4, i, :], xsm[i * 4:(i + 1) * 4, :], writes=[xtn])
                    else:
                        S.dma("sp", x4[:, i, :], xp[c * 128:(c + 1) * 128, :], writes=[xtn])
                    rms_T(x4[:, i, :], xtn, g_a[:], "g_a", fT[:, :, i * 128:(i + 1) * 128], "fT")
                chk("rms")
                for j in range(6):
                    wt, wtn = w_get()
                    for q in range(4):
                        cc = j * 4 + q
                        ps, psn = pf.next()
                        for k in range(8):
                            S.op("pe", lambda e, k=k: e.matmul(ps[:], lhsT=wt[:, k, q * 128:(q + 1) * 128], rhs=fT[:, k, :], start=(k == 0), stop=(k == 7)),
                                 reads=[wtn, "fT"], writes=[psn])
                        raw, rawn = rawr.next()
                        S.op("act", lambda e: e.activation(out=raw[:, :, 3:131], in_=ps[:].rearrange("p (i t) -> p i t", t=128), func=AF.Copy),
                             reads=[psn], writes=[rawn])
                        if is_samp:
                            S.op("pool", lambda e: e.tensor_copy(out=raw[:, :, 0:3], in_=sconvT[:, cc, :, :]), reads=["sconvT"], writes=[rawn])
                            S.op("pool", lambda e: e.tensor_copy(out=csamp[:, cc, :, :], in_=raw[:, :, 4:7]), reads=[rawn], writes=["csamp"])
                        else:
                            S.op("pool", lambda e: e.tensor_copy(out=raw[:, 0, 0:3], in_=tails[:, cc, :]), reads=["tails"], writes=[rawn])
                            S.op("pool", lambda e: e.tensor_copy(out=raw[:, 1:4, 0:3], in_=raw[:, 0:3, 128:131]), reads=[rawn], writes=[rawn])
                            S.op("pool", lambda e: e.tensor_copy(out=tails[:, cc, :], in_=raw[:, 3, 128:131]), reads=[rawn], writes=["tails"])
                        acc, accn = accr.next()
                        S.op("act", lambda e: e.activation(out=acc[:], in_=raw[:, :, 3:131], func=AF.Identity, scale=cw[:, cc, 3:4], bias=cb_[:, cc:cc + 1]),
                             reads=[rawn, "cw", "cb_"], writes=[accn])
                        for kk, eng in ((0, "dve"), (1, "dve"), (2, "dve")):
                            S.op(eng, lambda e, kk=kk: e.scalar_tensor_tensor(out=acc[:], in0=raw[:, :, kk:kk + 128], scalar=cw[:, cc, kk:kk + 1], in1=acc[:],
                                                                              op0=ALU.mult, op1=ALU.add), reads=[rawn, "cw", accn], writes=[accn])
                        S.op("act", lambda e: e.activation(out=xcT[:, cc, :].rearrange("p (i t) -> p i t", t=128), in_=acc[:], func=AF.Silu),
                             reads=[accn], writes=["xcT"])
                def conv_out(src_of_cc, srcname, n, dst2d, tag):
                    for q6 in range(6):
                        pcv, pcvn = pf.next()
                        for u in range(4):
                            S.op("pe", lambda e, u=u: e.transpose(out=pcv[0:n, u * 128:(u + 1) * 128], in_=src_of_cc(q6 * 4 + u), identity=ident_f[:]),
                                 reads=[srcname, "ident_f"], writes=[pcvn])
                        stg_, stgn_ = accr.next()
                        S.op("dve", lambda e: e.tensor_copy(out=stg_[0:n].rearrange("p i t -> p (i t)"), in_=pcv[0:n, 0:512]), reads=[pcvn], writes=[stgn_])
                        S.dma("sp", dst2d[:, q6 * 512:(q6 + 1) * 512], stg_[0:n].rearrange("p i t -> p (i t)"), reads=[stgn_], writes=["%s%d" % (tag, q6)])
                if st == last_prompt_st:
                    conv_out(lambda cc: tails[:, cc, :], "tails", 3, conv_p, "conv_p")
                if is_samp:
                    conv_out(lambda cc: csamp[:, cc, :, :].rearrange("p b r -> p (b r)"), "csamp", 12, conv_s.rearrange("b r c -> (b r) c"), "conv_s")
                chk("xbc")
                for j in range(4):
                    wt, wtn = w_get()
                    for i in range(4):
                        ps, psn = pf.next()
                        for k in range(8):
                            S.op("pe", lambda e, k=k: e.matmul(ps[:], lhsT=fT[:, k, i * 128:(i + 1) * 128], rhs=wt[:, k, :], start=(k == 0), stop=(k == 7)),
                                 reads=[wtn, "fT"], writes=[psn])
                        S.op("act", lambda e: e.activation(out=zs[:, i, j * 512:(j + 1) * 512], in_=ps[:], func=AF.Silu), reads=[psn], writes=["zs%d" % i])
                chk("z")
                wo = [None] * 4
                cctx = []
                for i in range(4):
                    c = st * 4 + i
                    sm, smn = smr.next()
                    dtr, Ah, cs, ctot, tmp, dec_in, dtdec, ea, decb, dt_ = (sm[:, q, :] for q in range(10))
                    ps, psn = pf.next()
                    for k in range(8):
                        S.op("pe", lambda e, k=k: e.matmul(ps[:, 0:NH], lhsT=fT[:, k, i * 128:(i + 1) * 128], rhs=wdt[:, k, :], start=(k == 0), stop=(k == 7)),
                             reads=["wdt", "fT"], writes=[psn])
                    S.op("dve", lambda e: e.tensor_tensor(out=dtr, in0=ps[:, 0:NH], in1=dtb_bc[:], op=ALU.add), reads=[psn, "dtb_bc"], writes=[smn])
                    S.op("dve", lambda e: e.scalar_tensor_tensor(out=tmp, in0=dtr, scalar=-1.0, in1=dtr, op0=ALU.mult, op1=ALU.max), reads=[smn], writes=[smn])
                    S.op("act", lambda e: e.activation(out=tmp, in_=tmp, func=AF.Exp, scale=-1.0), reads=[smn], writes=[smn])
                    S.op("dve", lambda e: e.tensor_scalar(out=tmp, in0=tmp, scalar1=1.0, scalar2=None, op0=ALU.add), reads=[smn], writes=[smn])
                    S.op("act", lambda e: e.activation(out=tmp, in_=tmp, func=AF.Ln), reads=[smn], writes=[smn])
                    S.op("dve", lambda e: e.scalar_tensor_tensor(out=dt_, in0=dtr, scalar=0.0, in1=tmp, op0=ALU.max, op1=ALU.add), reads=[smn], writes=[smn])
                    if is_samp:
                        S.op("dve", lambda e: e.tensor_scalar(out=dt_, in0=dt_, scalar1=tmask[:, 0:1], scalar2=None, op0=ALU.mult), reads=[smn, "tmask"], writes=[smn])
                    S.op("dve", lambda e: e.tensor_tensor(out=Ah, in0=dt_, in1=aneg_bc[:], op=ALU.mult), reads=[smn, "aneg_bc"], writes=[smn])
                    ps2, ps2n = pf.next()
                    S.op("pe", lambda e: e.matmul(ps2[:, 0:NH], lhsT=Umask[:], rhs=Ah, start=True, stop=True), reads=["Umask", smn], writes=[ps2n])
                    S.op("pe", lambda e: e.matmul(ps2[:, NH:2 * NH], lhsT=ones_f[:], rhs=Ah, start=True, stop=True), reads=["ones_f", smn], writes=[ps2n])
                    S.op("dve", lambda e: e.tensor_copy(out=cs, in_=ps2[:, 0:NH]), reads=[ps2n], writes=[smn])
                    S.op("dve", lambda e: e.tensor_tensor(out=dec_in, in0=ps2[:, NH:2 * NH], in1=cs, op=ALU.subtract), reads=[ps2n, smn], writes=[smn])
                    S.op("act", lambda e: e.activation(out=dec_in, in_=dec_in, func=AF.Exp), reads=[smn], writes=[smn])
                    S.op("act", lambda e: e.activation(out=decb, in_=ps2[:, NH:2 * NH], func=AF.Exp), reads=[ps2n], writes=[smn])
                    S.op("act", lambda e: e.activation(out=ea, in_=cs, func=AF.Exp), reads=[smn], writes=[smn])
                    S.op("dve", lambda e: e.tensor_tensor(out=dtdec, in0=dt_, in1=dec_in, op=ALU.mult), reads=[smn], writes=[smn])

                    ahl, ahln = ahlr.next()
                    S.op("dve", lambda e: e.tensor_copy(out=ahl[:, 0, :], in_=Ah), reads=[smn], writes=[ahln])
                    S.op("dve", lambda e: e.tensor_tensor(out=ctot, in0=Ah, in1=ahl[:, 0, :], op=ALU.subtract), reads=[smn, ahln], writes=[smn])
                    S.op("dve", lambda e: e.tensor_copy(out=ahl[:, 1, :], in_=ctot), reads=[smn], writes=[ahln])
                    cctx.append((sm, smn, ahl, ahln))

                def make_chunk(i):
                    c = st * 4 + i
                    sm, smn, ahl, ahln = cctx[i]
                    gR = {}
                    dtr, Ah, cs, ctot, tmp, dec_in, dtdec, ea, decb, dt_ = (sm[:, q, :] for q in range(10))

                    def state_load():
                        if is_samp:
                            S.dma("sp", ybuf[:].rearrange("q (j n) -> q j n", n=128), sssm[i].rearrange("(j q) n -> q j n", q=128), writes=["ybuf0", "ybuf1", "ybuf2", "ybuf3"])
                            for jj in range(4):
                                pst, pstn = pf.next()
                                for u in range(4):
                                    j2 = jj * 4 + u
                                    S.op("pe", lambda e, j2=j2, u=u: e.transpose(out=pst[:, u * 128:(u + 1) * 128], in_=ybuf[:, j2 * 128:(j2 + 1) * 128], identity=ident_f[:]),
                                         reads=["ybuf%d" % jj, "ident_f"], writes=[pstn])
                                S.op("dve", lambda e: e.tensor_copy(out=ST[:, jj * 512:(jj + 1) * 512], in_=pst[:]), reads=[pstn], writes=["ST%d" % jj])
                                S.op("act", lambda e: e.activation(out=STb[:, jj * 512:(jj + 1) * 512], in_=pst[:], func=AF.Copy), reads=[pstn], writes=["STb%d" % jj])

                        return None

                    tk = slice(i * 128, (i + 1) * 128)
                    gst = {}

                    def ssd_R(g):
                        Rg, Rgn = Rr.next()
                        for hl in range(2):
                            S.op("pool", lambda e, hl=hl: e.tensor_tensor(out=Rg[:, hl, :, :], in0=Umask_b[:].unsqueeze(1).to_broadcast([128, 8, 128]),
                                                                         in1=ahl[:, hl, g * 8:(g + 1) * 8].unsqueeze(2).to_broadcast([128, 8, 128]), op=ALU.mult),
                                 reads=["Umask_b", ahln], writes=[Rgn])
                        gR[g] = (Rg, Rgn)

                    def ssd_front(g):
                        pt, ptn = pb.next()
                        for u in range(4):
                            S.op("pe", lambda e, u=u: e.transpose(out=pt[:, u * 128:(u + 1) * 128], in_=xcT[:, g * 4 + u, tk], identity=ident_b[:]),
                                 reads=["xcT", "ident_b"], writes=[ptn])
                        S.op("pe", lambda e: e.transpose(out=pt[:, 512:640], in_=xcT[:, 16 + g, tk], identity=ident_b[:]), reads=["xcT", "ident_b"], writes=[ptn])
                        chk("g_pe")
                        Xg, Xgn = Xr.next()
                        Xd, Xdn = Xdr.next()
                        xsD, xsDn = xsDr.next()
                        Bt, Btn = Btr.next()
                        pt3 = pt[:, 0:512].rearrange("p (h d) -> p h d", d=HP)

                        def bc8(ap2):
                            return ap2[:, g * 8:(g + 1) * 8].unsqueeze(2).to_broadcast([128, 8, HP])
                        S.op("dve", lambda e: e.tensor_tensor(out=Xg[:].rearrange("p (h d) -> p h d", d=HP), in0=pt3, in1=bc8(dt_), op=ALU.mult), reads=[ptn, smn], writes=[Xgn])
                        chk("g_ev1")
                        S.op("dve", lambda e: e.tensor_tensor(out=Xd[:].rearrange("p (h d) -> p h d", d=HP), in0=pt3, in1=bc8(dtdec), op=ALU.mult), reads=[ptn, smn], writes=[Xdn])
                        chk("g_ev2")
                        S.op("dve", lambda e: e.tensor_tensor(out=xsD[:].rearrange("p (h d) -> p h d", d=HP), in0=pt3, in1=bc8(D_bc), op=ALU.mult), reads=[ptn, "D_bc"], writes=[xsDn])
                        chk("g_ev3")
                        S.op("dve", lambda e: e.tensor_copy(out=Bt[:], in_=pt[:, 512:640]), reads=[ptn], writes=[Btn])
                        chk("g_tr")
                        pc, pcn = pf.next()
                        S.op("pe", lambda e: e.matmul(pc[:, 0:128], lhsT=xcT[:, 16 + g, tk], rhs=xcT[:, 20 + g, tk], start=True, stop=True), reads=["xcT"], writes=[pcn])
                        cbm, cbmn = cbmr.next()
                        S.op("dve", lambda e: e.tensor_tensor(out=cbm[:], in0=pc[:, 0:128], in1=Umask[:], op=ALU.mult), reads=[pcn, "Umask"], writes=[cbmn])
                        chk("g_cb")
                        Rg, Rgn = gR[g]
                        chk("g_R")
                        mT, mTn = mTr.next()
                        for hh in range(2):
                            pseg, psegn = pf.next()
                            for hl in range(2):
                                S.op("pe", lambda e, hh=hh, hl=hl: e.matmul(pseg[:], lhsT=Lmask_b[:], rhs=Rg[:, hl, hh * 4:(hh + 1) * 4, :], start=(hl == 0), stop=(hl == 1)),
                                     reads=["Lmask_b", Rgn], writes=[psegn])
                            Et, Etn = Er.next()
                            S.op("act", lambda e: e.activation(out=Et[:], in_=pseg[:], func=AF.Exp), reads=[psegn], writes=[Etn])
                            S.op("dve", lambda e, hh=hh: e.tensor_tensor(out=mT[:, hh * 4:(hh + 1) * 4, :], in0=Et[:].rearrange("p (h t) -> p h t", t=128),
                                                                         in1=cbm[:].unsqueeze(1).to_broadcast([128, 4, 128]), op=ALU.mult),
                                 reads=[Etn, cbmn], writes=[mTn])
                        gst[g] = (Xg, Xgn, Xd, Xdn, xsD, xsDn, Bt, Btn, mT, mTn)

                    def ssd_back(g):
                        Xg, Xgn, Xd, Xdn, xsD, xsDn, Bt, Btn, mT, mTn = gst[g]

                        def bc8(ap2):
                            return ap2[:, g * 8:(g + 1) * 8].unsqueeze(2).to_broadcast([128, 8, HP])
                        chk("g_seg")
                        pyo, pyon = pf.next()
                        S.op("pe", lambda e: e.matmul(pyo[:], lhsT=xcT[:, 20 + g, tk], rhs=STb[:, g * 512:(g + 1) * 512], start=True, stop=True),
                             reads=["xcT", "STb%d" % g], writes=[pyon])
                        pyd, pydn = pf.next()
                        S.op("pe", lambda e: e.matmul(pyd[:], lhsT=ident_b[:], rhs=xsD[:], start=True, stop=False), reads=["ident_b", xsDn], writes=[pydn])
                        for h8 in range(8):
                            S.op("pe", lambda e, h8=h8: e.matmul(pyd[:, h8 * HP:(h8 + 1) * HP], lhsT=mT[:, h8, :], rhs=Xg[:, h8 * HP:(h8 + 1) * HP], start=False, stop=True),
                                 reads=[mTn, Xgn], writes=[pydn])
                        yg_ = ybuf[:, g * 512:(g + 1) * 512]
                        S.op("dve", lambda e: e.tensor_tensor(out=yg_.rearrange("p (h d) -> p h d", d=HP), in0=pyo[:].rearrange("p (h d) -> p h d", d=HP),
                                                              in1=bc8(ea), op=ALU.mult), reads=[pyon, smn], writes=["ybuf%d" % g])
                        S.op("dve", lambda e: e.tensor_tensor(out=yg_, in0=pyd[:], in1=yg_, op=ALU.add), reads=[pydn, "ybuf%d" % g], writes=["ybuf%d" % g])
                        chk("g_y")
                        pcs, pcsn = pf.next()
                        S.op("pe", lambda e: e.matmul(pcs[:], lhsT=Bt[:], rhs=Xd[:], start=True, stop=True), reads=[Btn, Xdn], writes=[pcsn])
                        chk("g_cs")
                        STg = ST[:, g * 512:(g + 1) * 512]
                        S.op("pool", lambda e: e.tensor_tensor(out=STg.rearrange("p (h d) -> p h d", d=HP), in0=STg.rearrange("p (h d) -> p h d", d=HP),
                                                               in1=bc8(decb), op=ALU.mult), reads=["ST%d" % g, smn], writes=["ST%d" % g])
                        S.op("dve", lambda e: e.tensor_tensor(out=STg, in0=pcs[:], in1=STg, op=ALU.add), reads=[pcsn, "ST%d" % g], writes=["ST%d" % g])
                        S.op("act", lambda e: e.activation(out=STb[:, g * 512:(g + 1) * 512], in_=STg, func=AF.Copy), reads=["ST%d" % g], writes=["STb%d" % g])
                        chk("g_end")


                    def gate():
                        S.op("dve", lambda e: e.tensor_tensor(out=ygb[:], in0=ybuf[:], in1=zs[:, i, :], op=ALU.mult), reads=["ybuf0", "ybuf1", "ybuf2", "ybuf3"] + ["zs%d" % i], writes=["ygb"])
                        ss2, ss2n = ss2r.next()
                        S.op("act", lambda e: e.activation(out=junk[:], in_=ygb[:, 0:1024], func=AF.Square, accum_out=ss2[:, 0:1]), reads=["ygb"], writes=["junk", ss2n])
                        S.op("act", lambda e: e.activation(out=junk[:], in_=ygb[:, 1024:2048], func=AF.Square, accum_out=ss2[:, 1:2]), reads=["ygb"], writes=["junk", ss2n])
                        S.op("dve", lambda e: e.tensor_tensor(out=ss2[:, 0:1], in0=ss2[:, 0:1], in1=ss2[:, 1:2], op=ALU.add), reads=[ss2n], writes=[ss2n])
                        S.op("dve", lambda e: e.tensor_scalar(out=ss2[:, 1:2], in0=ss2[:, 0:1], scalar1=1.0 / DI, scalar2=EPS, op0=ALU.mult, op1=ALU.add), reads=[ss2n], writes=[ss2n])
                        S.op("act", lambda e: e.activation(out=ss2[:, 2:3], in_=ss2[:, 1:2], func=AF.Ln), reads=[ss2n], writes=[ss2n])
                        S.op("act", lambda e: e.activation(out=ss2[:, 3:4], in_=ss2[:, 2:3], func=AF.Exp, scale=-0.5), reads=[ss2n], writes=[ss2n])
                        wo[i] = (ss2, ss2n)
                        for hf in range(2):
                            pt, ptn = pb.next()
                            for k in range(8):
                                kk = hf * 8 + k
                                S.op("pe", lambda e, k=k, kk=kk: e.transpose(out=pt[:, k * 128:(k + 1) * 128], in_=ygb[:, kk * 128:(kk + 1) * 128], identity=ident_b[:]),
                                     reads=["ygb", "ident_b"], writes=[ptn])
                            S.op("dve", lambda e: e.tensor_tensor(out=ygT[:, hf * 8:(hf + 1) * 8, tk], in0=pt[:].rearrange("p (k t) -> p k t", t=128),
                                                                  in1=g_gate[:, hf * 8:(hf + 1) * 8].unsqueeze(2).to_broadcast([128, 8, 128]), op=ALU.mult),
                                 reads=[ptn, "g_gate"], writes=["ygT"])

                        if is_samp or c == final_state_at:
                            dst = ssm_s[i] if is_samp else ssm_p
                            for jj in range(4):
                                pst, pstn = pf.next()
                                for u in range(4):
                                    j2 = jj * 4 + u
                                    S.op("pe", lambda e, j2=j2, u=u: e.transpose(out=pst[:, u * 128:(u + 1) * 128], in_=ST[:, j2 * 128:(j2 + 1) * 128], identity=ident_f[:]),
                                         reads=["ST%d" % jj, "ident_f"], writes=[pstn])
                                S.op("dve", lambda e: e.tensor_copy(out=ybuf[:, jj * 512:(jj + 1) * 512], in_=pst[:]), reads=[pstn], writes=["ybuf%d" % jj])
                            S.dma("sp", dst.rearrange("(j q) n -> q j n", q=128), ybuf[:].rearrange("q (j n) -> q j n", n=128), reads=["ybuf0", "ybuf1", "ybuf2", "ybuf3"], writes=["ssm_out"])

                    return ssd_front, ssd_back, gate, state_load, ssd_R

                chunks = [make_chunk(i) for i in range(4)]
                seq = [(i, g) for i in range(4) for g in range(NG)]
                chunks[0][4](0)
                chunks[0][4](1)
                chunks[0][0](0)
                for k, (i, g) in enumerate(seq):
                    if k + 2 < len(seq):
                        chunks[seq[k + 2][0]][4](seq[k + 2][1])
                    if k + 1 < len(seq):
                        chunks[seq[k + 1][0]][0](seq[k + 1][1])
                    if g == 0 and is_samp:
                        chunks[i][3]()
                    chunks[i][1](g)
                    if g == NG - 1:
                        chunks[i][2]()
                chk("gate")
                for hf in range(2):
                    w0, w0n = w_get()
                    w1, w1n = w_get()
                    for i in range(4):
                        c = st * 4 + i
                        xtn = "x4_%d" % i
                        ps, psn = pf.next()
                        for kk in range(16):
                            wt, wtn = (w0, w0n) if kk < 8 else (w1, w1n)
                            S.op("pe", lambda e, kk=kk, wt=wt: e.matmul(ps[:], lhsT=ygT[:, kk, i * 128:(i + 1) * 128], rhs=wt[:, kk % 8, :], start=(kk == 0), stop=(kk == 15)),
                                 reads=["ygT", wtn], writes=[psn])
                        ss2, ss2n = wo[i]
                        hsl = x4[:, i, hf * 512:(hf + 1) * 512]
                        S.op("dve", lambda e: e.scalar_tensor_tensor(out=hsl, in0=ps[:], scalar=ss2[:, 3:4], in1=hsl, op0=ALU.mult, op1=ALU.add),
                             reads=[psn, ss2n, xtn], writes=[xtn])
                        if hf == 1:
                            S.dma("sp", hAscr[c * 128:(c + 1) * 128, :], x4[:, i, :], reads=[xtn], writes=["hAscr"])
                            if dbg and not is_samp and c < 16:
                                S.dma("sp", yp[c * 128:(c + 1) * 128, :], x4[:, i, :], reads=[xtn], writes=["yp"])
                            rms_T(x4[:, i, :], xtn, g_kv[:], "g_kv", fT[:, :, i * 128:(i + 1) * 128], "fT")
                chk("outproj")
                for j in range(4):
                    wt, wtn = w_get()
                    for i in range(4):
                        c = st * 4 + i
                        ps, psn = pf.next()
                        for k in range(8):
                            S.op("pe", lambda e, k=k: e.matmul(ps[:], lhsT=fT[:, k, i * 128:(i + 1) * 128], rhs=wt[:, k, :], start=(k == 0), stop=(k == 7)),
                                 reads=[wtn, "fT"], writes=[psn])
                        kvb, kvbn = kvbr.next()
                        S.op("act", lambda e: e.activation(out=kvb[:], in_=ps[:], func=AF.Copy), reads=[psn], writes=[kvbn])
                        if is_samp or c >= NCHP - 16:
                            kvt, kvtn = kvtr.next()
                            S.op("act", lambda e: e.activation(out=kvt[:], in_=ps[:], func=AF.Copy), reads=[psn], writes=[kvtn])
                            if is_samp:
                                S.dma("sp", kv_s[i * 4:(i + 1) * 4, j * 512:(j + 1) * 512], kvt[0:4, :], reads=[kvtn], writes=["kv_out"])
                            else:
                                r0 = (c - (NCHP - 16)) * 128
                                S.dma("sp", kv_p[r0:r0 + 128, j * 512:(j + 1) * 512], kvt[:], reads=[kvtn], writes=["kv_out"])
                        pt, ptn = pb.next()
                        for u in range(4):
                            S.op("pe", lambda e, u=u: e.transpose(out=pt[:, u * 128:(u + 1) * 128], in_=kvb[:, u * 128:(u + 1) * 128], identity=ident_b[:]),
                                 reads=[kvbn, "ident_b"], writes=[ptn])
                        half = i // 2
                        S.op("dve", lambda e: e.tensor_copy(out=kTst[:, j * 4:(j + 1) * 4, (i % 2) * 128:(i % 2 + 1) * 128],
                                                            in_=pt[:, 0:512].rearrange("p (u t) -> p u t", t=128)),
                             reads=[ptn], writes=["kTst%d_%d" % (j, i % 2)])
                        if i % 2 == 1:
                            t0 = st * 512 + half * 256
                            S.dma("sp", kvT[j * 4:(j + 1) * 4, :, t0:t0 + 256].rearrange("u p t -> p u t"), kTst[:, j * 4:(j + 1) * 4, :],
                                  reads=["kTst%d_0" % j, "kTst%d_1" % j], writes=["kvT"])

            S.barrier()
            es1.close()
            chk("p2")

            def sb3(name, shape, dt=F32):
                return sb(name, shape, dt, es=es3)

            SCL = 1.0 / math.sqrt(128.0)

            def sb3b(name, shape, dt=F32):
                return sb(name, shape, dt, es=es3b)

            gsil = sb3("gsil", [128, 2048], BF16)
            Btl = sb3("Btl", [128, 24, 256])
            ogT = sb3("ogT", [128, 8, 2048], BF16)
            hBr = Ring(es3, nc, "hB", [128, D], F32, 4)
            fn_bc = sb3("fn_bc", [128, D])
            ones_b = sb3("ones_b", [128, 128], BF16)
            stat = sb3("stat", [128, 32])
            sflag = sb3("sflag", [128, 2])
            hnT = sb3b("hnT", [128, 8, 2048], BF16)
            QT = sb3b("QT", [128, 3, 2048], BF16)
            KTw = sb3b("KTw", [128, 4096], BF16)
            VTw = sb3b("VTw", [128, 4096], BF16)
            acc = sb3b("acc", [128, 2, 2048])
            sq = sb3b("sq", [128, 2048], BF16)
            stg = sb3b("stg", [128, 2048], BF16)
            tmpr = Ring(es3b, nc, "tmpS", [128, 256], F32, 4)
            PTr = Ring(es3b, nc, "PT", [128, 256], BF16, 4)
            Vbr = Ring(es3b, nc, "Vb", [128, 128], BF16, 8)

            S.op("dve", lambda e: e.memset(ones_b[:], 1.0), writes=["ones_b"])
            with nc.allow_non_contiguous_dma(reason="small parameter loads"):
                S.dma("sp", fn_bc[:], final_norm.partition_broadcast(128), writes=["fn_bc"])
            S.dma("sp", sflag[:], spanflag, writes=["sflag"])
            with nc.allow_non_contiguous_dma(reason="Toeplitz bias tiles (one-time)"):
                for gh in range(24):
                    S.dma("pool", Btl[:, gh, 0:128], bass.AP(fbig_t, gh * 128 * 383 + 255, [[382, 128], [1, 128]]), reads=["fbig"], writes=["Btl"])
                    S.dma("pool", Btl[:, gh, 128:256], bass.AP(fbig_t, gh * 128 * 383 + 127, [[382, 128], [1, 128]]), reads=["fbig"], writes=["Btl"])

            def load_hA(ci):
                hA_, hAn_ = hBr.next()
                hB_, hBn_ = hBr.next()
                S.dma("sp", hA_[:], hAscr[ci * 128:(ci + 1) * 128, :], reads=["hAscr"], writes=[hAn_])
                S.dma("sp", hB_[:], hAscr[2048 + ci * 128:2048 + (ci + 1) * 128, :], reads=["hAscr"], writes=[hBn_])
                S.op("dve", lambda e: e.tensor_tensor(out=hB_[:], in0=hB_[:], in1=hA_[:], op=ALU.subtract), reads=[hAn_, hBn_], writes=[hBn_])
                S.op("dve", lambda e: e.scalar_tensor_tensor(out=hA_[:], in0=hB_[:], scalar=sflag[:, 0:1], in1=hA_[:], op0=ALU.mult, op1=ALU.add),
                     reads=[hAn_, hBn_, "sflag"], writes=[hAn_])
                return hA_, hAn_
            chk("p3c")

            def out_proj(T0, NT, is_samp):
                w0, w0n = w_get()
                w1, w1n = w_get()
                for ci in range(NT // 128):
                    if is_samp:
                        hB, hBn = hBr.next()
                        S.dma("sp", hB[:], hAscr[T0 + ci * 128:T0 + (ci + 1) * 128, :], reads=["hAscr"], writes=[hBn])
                    else:
                        hB, hBn = load_hA(ci)
                    for hf, (wt, wtn) in enumerate(((w0, w0n), (w1, w1n))):
                        ps, psn = pf.next()
                        for k in range(8):
                            S.op("pe", lambda e, k=k: e.matmul(ps[:], lhsT=ogT[:, k, ci * 128:(ci + 1) * 128], rhs=wt[:, k, :], start=(k == 0), stop=(k == 7)),
                                 reads=["ogT", wtn], writes=[psn])
                        S.op("dve", lambda e: e.tensor_tensor(out=hB[:, hf * 512:(hf + 1) * 512], in0=ps[:], in1=hB[:, hf * 512:(hf + 1) * 512], op=ALU.add), reads=[psn, hBn], writes=[hBn])
                    ss, ssn = ssr.next()
                    S.op("act", lambda e: e.activation(out=junk[:], in_=hB[:], func=AF.Square, accum_out=ss[:, 0:1]), reads=[hBn], writes=["junk", ssn])
                    S.op("dve", lambda e: e.tensor_scalar(out=ss[:, 1:2], in0=ss[:, 0:1], scalar1=1.0 / D, scalar2=EPS, op0=ALU.mult, op1=ALU.add), reads=[ssn], writes=[ssn])
                    S.op("act", lambda e: e.activation(out=ss[:, 2:3], in_=ss[:, 1:2], func=AF.Ln), reads=[ssn], writes=[ssn])
                    S.op("act", lambda e: e.activation(out=ss[:, 3:4], in_=ss[:, 2:3], func=AF.Exp, scale=-0.5), reads=[ssn], writes=[ssn])
                    S.op("dve", lambda e: e.scalar_tensor_tensor(out=hB[:], in0=hB[:], scalar=ss[:, 3:4], in1=fn_bc[:], op0=ALU.mult, op1=ALU.mult), reads=[hBn, ssn, "fn_bc"], writes=[hBn])
                    if is_samp:
                        S.dma("sp", ysm[ci * 4:(ci + 1) * 4, :], hB[0:4, :], reads=[hBn], writes=["y_out"])
                    else:
                        S.dma("sp", yp[ci * 128:(ci + 1) * 128, :], hB[:], reads=[hBn], writes=["y_out%d" % (ci % 2)])


            for sp in ([1] if (0 in spans or 1 in spans) else []):
                is_samp = False
                T0 = 0
                NT = 2048
                W = 4096
                for ci in range(NT // 128):
                    hB, hBn = load_hA(ci)
                    rms_T(hB[:], hBn, g_b[:], "g_b", hnT[:, :, ci * 128:(ci + 1) * 128], "hnT")
                chk("p3a")
                for h in range(8):
                    wt, wtn = w_get()
                    for (Win, Wn, hidx) in ((KTw, "KTw", h), (VTw, "VTw", 8 + h)):
                        S.dma("sp", sq[:], kvT[hidx, :, 0:2048], reads=["kvT"], writes=["sq"])
                        S.dma("sp", stg[:], kvT[hidx, :, 2048:4096], reads=["kvT"], writes=["stg"])
                        S.op("act", lambda e, Win=Win: e.activation(out=Win[:, 0:2048], in_=sq[:], func=AF.Copy, scale=sflag[:, 0:1]), reads=["sq", "sflag"], writes=[Wn])
                        S.op("dve", lambda e: e.tensor_tensor(out=stg[:], in0=stg[:], in1=sq[:], op=ALU.subtract), reads=["sq", "stg"], writes=["stg"])
                        S.op("dve", lambda e, Win=Win: e.scalar_tensor_tensor(out=Win[:, 2048:4096], in0=stg[:], scalar=sflag[:, 0:1], in1=sq[:], op0=ALU.mult, op1=ALU.add),
                             reads=["sq", "stg", "sflag"], writes=[Wn])
                    for tt in range(NT // 512):
                        tsl = slice(tt * 512, (tt + 1) * 512)
                        for q in range(4):
                            ps, psn = pf.next()
                            for k in range(8):
                                S.op("pe", lambda e, k=k: e.matmul(ps[:], lhsT=wt[:, k, q * 128:(q + 1) * 128], rhs=hnT[:, k, tsl], start=(k == 0), stop=(k == 7)),
                                     reads=[wtn, "hnT"], writes=[psn])
                            if q < 3:
                                S.op("dve", lambda e: e.tensor_scalar(out=QT[:, q, tsl], in0=ps[:], scalar1=SCL, scalar2=None, op0=ALU.mult), reads=[psn], writes=["QT"])
                            else:
                                S.op("act", lambda e: e.activation(out=gsil[:, tsl], in_=ps[:], func=AF.Silu), reads=[psn], writes=["gsil"])
                    nst = 0
                    for src, n2 in [(QT[:, 0, :], NT), (QT[:, 1, :], NT), (QT[:, 2, :], NT), (KTw[:, 0:2048], 2048)] + ([(KTw[:, 2048:4096], 2048)] if W > 2048 else []):
                        rn = ["QT"] if nst < 3 else ["KTw"]
                        S.op("act", lambda e: e.activation(out=sq[:, 0:n2], in_=src, func=AF.Square), reads=rn, writes=["sq"])
                        for tt in range(n2 // 512):
                            ps, psn = pf.next()
                            S.op("pe", lambda e: e.matmul(ps[:], lhsT=ones_b[:], rhs=sq[:, tt * 512:(tt + 1) * 512], start=True, stop=True), reads=["ones_b", "sq"], writes=[psn])
                            col = (0 if nst < 3 else 16) + (nst % 3 if nst < 3 else nst - 3) * 4 + tt
                            S.op("dve", lambda e, col=col: e.reduce_max(out=stat[:, col:col + 1], in_=ps[:], axis=AX.X), reads=[psn], writes=["stat"])
                        nst += 1
                    nkc = 16 + 4 * (nst - 3)
                    S.op("dve", lambda e: e.reduce_max(out=stat[:, 30:31], in_=stat[:, 0:12], axis=AX.X), reads=["stat"], writes=["stat"])
                    S.op("dve", lambda e: e.reduce_max(out=stat[:, 31:32], in_=stat[:, 16:nkc], axis=AX.X), reads=["stat"], writes=["stat"])
                    S.op("dve", lambda e: e.tensor_tensor(out=stat[:, 30:31], in0=stat[:, 30:31], in1=stat[:, 31:32], op=ALU.mult), reads=["stat"], writes=["stat"])
                    S.op("act", lambda e: e.activation(out=stat[:, 30:31], in_=stat[:, 30:31], func=AF.Ln), reads=["stat"], writes=["stat"])
                    S.op("act", lambda e: e.activation(out=stat[:, 30:31], in_=stat[:, 30:31], func=AF.Exp, scale=0.5), reads=["stat"], writes=["stat"])
                    S.op("dve", lambda e: e.scalar_tensor_tensor(out=stat[:, 29:30], in0=stat[:, 30:31], scalar=-1.02, in1=bmax[:, 0:1], op0=ALU.mult, op1=ALU.subtract),
                         reads=["stat", "bmax"], writes=["stat"])
                    S.op("dve", lambda e: e.tensor_scalar(out=stat[:, 29:30], in0=stat[:, 29:30], scalar1=-1.0, scalar2=None, op0=ALU.add), reads=["stat"], writes=["stat"])
                    negM = stat[:, 29:30]
                    S.op("pool", lambda e: e.memset(acc[:], 0.0), writes=["acc"])
                    tiles = []
                    for g, d in enumerate((1, 4, 16)):
                        for r in range(d):
                            for nb in range((2048 // d) // 128):
                                tiles.append((g, d, r, nb))
                    tst = {}

                    def stageA(i):
                        g, d, r, nb = tiles[i]
                        gh = g * 8 + h
                        nblk = (2048 // d) // 128
                        Qv = QT[:, g, :].rearrange("p (j d) -> p j d", d=d)
                        Kv = KTw[:].rearrange("p (j d) -> p j d", d=d)
                        Vv = VTw[:].rearrange("p (j d) -> p j d", d=d)
                        n = nblk + nb
                        has_prev = True
                        lo = 0
                        qa = Qv[:, nb * 128:(nb + 1) * 128, r]
                        ps, psn = pf.next()
                        if has_prev:
                            S.op("pe", lambda e: e.matmul(ps[:, 0:128], lhsT=Kv[:, (n - 1) * 128:n * 128, r], rhs=qa, start=True, stop=True), reads=["KTw", "QT"], writes=[psn])
                        S.op("pe", lambda e: e.matmul(ps[:, 128:256], lhsT=Kv[:, n * 128:(n + 1) * 128, r], rhs=qa, start=True, stop=True), reads=["KTw", "QT"], writes=[psn])
                        tmp, tmpn = tmpr.next()
                        if nb == 0:
                            S.op("dve", lambda e: e.scalar_tensor_tensor(out=tmp[:, 0:128], in0=ps[:, 0:128], scalar=sflag[:, 1:2], in1=Btl[:, gh, 0:128], op0=ALU.add, op1=ALU.add),
                                 reads=[psn, "Btl", "sflag"], writes=[tmpn])
                            S.op("dve", lambda e: e.tensor_tensor(out=tmp[:, 128:256], in0=ps[:, 128:256], in1=Btl[:, gh, 128:256], op=ALU.add), reads=[psn, "Btl"], writes=[tmpn])
                        else:
                            S.op("dve", lambda e: e.tensor_tensor(out=tmp[:, lo:256], in0=ps[:, lo:256], in1=Btl[:, gh, lo:256], op=ALU.add), reads=[psn, "Btl"], writes=[tmpn])
                        PT, PTn = PTr.next()
                        S.op("act", lambda e: e.activation(out=PT[:, lo:256], in_=tmp[:, lo:256], func=AF.Exp, bias=negM), reads=[tmpn, "stat"], writes=[PTn])

                        def vblock(nk):
                            pt, ptn = pb.next()
                            S.op("pe", lambda e: e.transpose(out=pt[:, 0:128], in_=Vv[:, nk * 128:(nk + 1) * 128, r], identity=ident_b[:]), reads=["VTw", "ident_b"], writes=[ptn])
                            vb, vbn = Vbr.next()
                            S.op("dve", lambda e: e.tensor_copy(out=vb[:], in_=pt[:, 0:128]), reads=[ptn], writes=[vbn])
                            return vb, vbn
                        vprev = None
                        if has_prev:
                            vprev = vblock(n - 1) if nb == 0 else tst[i - 1]["vcur"]
                        vcur = vblock(n)
                        tst[i] = dict(PT=PT, PTn=PTn, vprev=vprev, vcur=vcur, has_prev=has_prev)

                    def stageB(i):
                        g, d, r, nb = tiles[i]
                        t_ = tst[i]
                        PT, PTn, vprev, vcur, has_prev = t_["PT"], t_["PTn"], t_["vprev"], t_["vcur"], t_["has_prev"]
                        Av = acc[:].rearrange("p a (j d) -> p a j d", d=d)
                        po, pon = pf.next()
                        if has_prev:
                            S.op("pe", lambda e: e.matmul(po[:, 0:128], lhsT=vprev[0][:], rhs=PT[:, 0:128], start=True, stop=False), reads=[vprev[1], PTn], writes=[pon])
                        S.op("pe", lambda e: e.matmul(po[:, 0:128], lhsT=vcur[0][:], rhs=PT[:, 128:256], start=not has_prev, stop=True), reads=[vcur[1], PTn], writes=[pon])
                        if has_prev:
                            S.op("pe", lambda e: e.matmul(po[:, 128:256], lhsT=ones_b[:], rhs=PT[:, 0:128], start=True, stop=False), reads=["ones_b", PTn], writes=[pon])
                        S.op("pe", lambda e: e.matmul(po[:, 128:256], lhsT=ones_b[:], rhs=PT[:, 128:256], start=not has_prev, stop=True), reads=["ones_b", PTn], writes=[pon])
                        av = Av[:, :, nb * 128:(nb + 1) * 128, r]
                        S.op("dve", lambda e: e.tensor_tensor(out=av, in0=po[:, 0:256].rearrange("p (a q) -> p a q", a=2), in1=av, op=ALU.add), reads=[pon, "acc"], writes=["acc"])
                        if i >= 1:
                            tst.pop(i - 1, None)

                    APIPE = 2
                    for i in range(min(APIPE, len(tiles))):
                        stageA(i)
                    for i in range(len(tiles)):
                        if i + APIPE < len(tiles):
                            stageA(i + APIPE)
                        stageB(i)
                    S.op("dve", lambda e: e.reciprocal(out=acc[:, 1, :], in_=acc[:, 1, :]), reads=["acc"], writes=["acc"])
                    S.op("dve", lambda e: e.tensor_tensor(out=acc[:, 0, :], in0=acc[:, 0, :], in1=acc[:, 1, :], op=ALU.mult), reads=["acc"], writes=["acc"])
                    S.op("dve", lambda e: e.tensor_tensor(out=ogT[:, h, 0:NT], in0=acc[:, 0, :], in1=gsil[:, 0:NT], op=ALU.mult), reads=["acc", "gsil"], writes=["ogT"])
                    chk("p3h")
                out_proj(T0, NT, False)
                chk("p3s")
            S.barrier()
            es3b.close()
            if 2 in spans:
                es3c = es3

                def sbc(name, shape, dt=F32):
                    return sb(name, shape, dt, es=es3c)

                T0s = 4096
                hnTs = sbc("hnTs", [128, 8, 512], BF16)
                QTs = sbc("QTs", [128, 24, 16], BF16)
                gsils = sbc("gsils", [128, 8, 16])
                KTn = sbc("KTn", [128, 8, 16], BF16)
                KTs = sbc("KTs", [128, 4, 8, 128], BF16)
                Vs = sbc("Vs", [128, 4, 1024], BF16)
                Vnb = sbc("Vnb", [4, 1024], BF16)
                Og = sbc("Og", [4, 3, 8, 128])
                mlt = sbc("mlt", [4, 8, 24])
                browrep = sbc("browrep", [4, 24, 128])
                BsA = sbc("BsA", [4, 8, 128])
                Bn = sbc("Bn", [4, 24, 4])
                negoff = sbc("negoff", [4, 4])
                tSr = Ring(es3c, nc, "tS", [4, 516], F32, 4)
                eSr = Ring(es3c, nc, "eS", [4, 516], BF16, 4)
                eTr = Ring(es3c, nc, "eT", [128, 20], BF16, 3)
                osb = sbc("osb", [4, 8, 128])

                S.op("dve", lambda e: e.tensor_scalar(out=negoff[:], in0=ident_f[0:4, 0:4], scalar1=-NEG, scalar2=NEG, op0=ALU.mult, op1=ALU.add),
                     reads=["ident_f"], writes=["negoff"])
                for gh in range(24):
                    pt_, ptn_ = pf.next()
                    if gh < 8:
                        S.op("pe", lambda e: e.transpose(out=pt_[0:4, 0:128], in_=Btl[:, gh, 0:4], identity=ident_f[:]), reads=["Btl", "ident_f"], writes=[ptn_])
                        S.op("dve", lambda e: e.tensor_copy(out=BsA[:, gh, :], in_=pt_[0:4, 0:128]), reads=[ptn_], writes=["BsA"])
                        S.op("pe", lambda e: e.transpose(out=pt_[0:4, 128:132], in_=Btl[0:4, gh, 128:132], identity=ident_f[0:4, 0:4]), reads=["Btl", "ident_f"], writes=[ptn_])
                        S.op("dve", lambda e: e.tensor_copy(out=Bn[:, gh, :], in_=pt_[0:4, 128:132]), reads=[ptn_], writes=["Bn"])
                    else:
                        S.op("pe", lambda e: e.transpose(out=pt_[0:4, 0:128], in_=Btl[:, gh, 0:1].to_broadcast([128, 4]), identity=ident_f[:]), reads=["Btl", "ident_f"], writes=[ptn_])
                        S.op("dve", lambda e: e.tensor_copy(out=browrep[:, gh, :], in_=pt_[0:4, 0:128]), reads=[ptn_], writes=["browrep"])
                        S.op("pe", lambda e: e.transpose(out=pt_[0:4, 128:132], in_=Btl[0:4, gh, 128:129].to_broadcast([4, 4]), identity=ident_f[0:4, 0:4]), reads=["Btl", "ident_f"], writes=[ptn_])
                        S.op("dve", lambda e: e.tensor_tensor(out=Bn[:, gh, :], in0=pt_[0:4, 128:132], in1=ident_f[0:4, 0:4], op=ALU.mult), reads=[ptn_, "ident_f"], writes=["Bn"])
                        S.op("dve", lambda e: e.tensor_tensor(out=Bn[:, gh, :], in0=Bn[:, gh, :], in1=negoff[:], op=ALU.add), reads=["Bn", "negoff"], writes=["Bn"])
                chk("s_bias")
                for ci in range(4):
                    hB, hBn = hBr.next()
                    S.dma("sp", hB[:], hAscr[T0s + ci * 128:T0s + (ci + 1) * 128, :], reads=["hAscr"], writes=[hBn])
                    rms_T(hB[:], hBn, g_b[:], "g_b", hnTs[:, :, ci * 128:(ci + 1) * 128], "hnTs")
                with nc.allow_non_contiguous_dma(reason="new K^T columns (tiny)"):
                    for h in range(8):
                        S.dma("sp", KTn[:, h, :].rearrange("p (b t) -> p b t", t=4),
                              kvT[h, :, T0s:T0s + 512].rearrange("p (b t) -> p b t", t=128)[:, :, 0:4], reads=["kvT"], writes=["KTn"])
                for h in range(8):
                    wt, wtn = w_get()
                    ps, psn = pf.next()
                    for q in range(4):
                        for k in range(8):
                            S.op("pe", lambda e, k=k, q=q: e.matmul(ps[:, q * 16:(q + 1) * 16], lhsT=wt[:, k, q * 128:(q + 1) * 128],
                                                                    rhs=hnTs[:, k, :].rearrange("p (b t) -> p b t", t=128)[:, :, 0:4], start=(k == 0), stop=(k == 7)),
                                 reads=[wtn, "hnTs"], writes=[psn])
                    for q in range(3):
                        S.op("dve", lambda e, q=q: e.tensor_scalar(out=QTs[:, q * 8 + h, :], in0=ps[:, q * 16:(q + 1) * 16], scalar1=SCL, scalar2=None, op0=ALU.mult), reads=[psn], writes=["QTs"])
                    S.op("act", lambda e: e.activation(out=gsils[:, h, :], in_=ps[:, 48:64], func=AF.Silu), reads=[psn], writes=["gsils"])
                chk("s_q")
                for b in range(NSB):
                    hBv, hBvn = hBr.next()
                    S.dma("sp", hBv[0:4, :], kv_s[b * 4:(b + 1) * 4, 1024:2048], reads=["kv_out"], writes=[hBvn])
                    S.op("dve", lambda e: e.tensor_copy(out=Vnb[:], in_=hBv[0:4, :]), reads=[hBvn], writes=["Vnb"])
                    for g, d in enumerate((1, 4, 16)):
                        nblk = 1 if g == 0 else 4
                        ckb = ckv[b].rearrange("(j d) c -> j d c", d=d)
                        for t in range(nblk):
                            j0 = 1920 if g == 0 else (384 if g == 1 else 0)
                            rsel = 0 if g == 0 else t
                            Kf, Kfn = hBr.next()
                            S.dma("sp", Kf[:], ckb[j0:j0 + 128, rsel, 0:1024], writes=[Kfn])
                            Kb, Kbn = xsbr.next()
                            S.op("dve", lambda e: e.tensor_copy(out=Kb[:], in_=Kf[:]), reads=[Kfn], writes=[Kbn])
                            ptk, ptkn = pb.next()
                            for hh in range(8):
                                S.op("pe", lambda e, hh=hh: e.transpose(out=ptk[:, hh * 128:(hh + 1) * 128], in_=Kb[:, hh * 128:(hh + 1) * 128], identity=ident_b[:]),
                                     reads=[Kbn, "ident_b"], writes=[ptkn])
                            S.op("dve", lambda e: e.tensor_copy(out=KTs[:, t, :, :], in_=ptk[:].rearrange("p (h k) -> p h k", k=128)), reads=[ptkn], writes=["KTs"])
                            Vf, Vfn = hBr.next()
                            S.dma("sp", Vf[:], ckb[j0:j0 + 128, rsel, 1024:2048], writes=[Vfn])
                            S.op("pool", lambda e: e.tensor_copy(out=Vs[:, t, :], in_=Vf[:]), reads=[Vfn], writes=["Vs"])
                        sst = {}

                        def sA(h):
                            gh = g * 8 + h
                            ncol = nblk * 128
                            qa = QTs[:, gh, b * 4:(b + 1) * 4]
                            ps, psn = pf.next()
                            for t in range(nblk):
                                S.op("pe", lambda e, t=t: e.matmul(ps[0:4, t * 128:(t + 1) * 128], lhsT=qa, rhs=KTs[:, t, h, :], start=True, stop=True), reads=["QTs", "KTs"], writes=[psn])
                            ps2, ps2n = pf.next()
                            S.op("pe", lambda e: e.matmul(ps2[0:4, 0:4], lhsT=qa, rhs=KTn[:, h, b * 4:(b + 1) * 4], start=True, stop=True), reads=["QTs", "KTn"], writes=[ps2n])
                            tS, tSn = tSr.next()
                            if g == 0:
                                S.op("dve", lambda e: e.tensor_tensor(out=tS[:, 0:128], in0=ps[0:4, 0:128], in1=BsA[:, h, :], op=ALU.add), reads=[psn, "BsA"], writes=[tSn])
                            else:
                                S.op("dve", lambda e: e.tensor_tensor(out=tS[:, 0:512].rearrange("p (t k) -> p t k", k=128), in0=ps[0:4, 0:512].rearrange("p (t k) -> p t k", k=128),
                                                                      in1=browrep[:, gh, :].unsqueeze(1).to_broadcast([4, 4, 128]), op=ALU.add), reads=[psn, "browrep"], writes=[tSn])
                                S.op("dve", lambda e: e.tensor_tensor(out=tS[:, 0:512].rearrange("p (t k) -> p t k", k=128), in0=tS[:, 0:512].rearrange("p (t k) -> p t k", k=128),
                                                                      in1=negoff[:].unsqueeze(2).to_broadcast([4, 4, 128]), op=ALU.add), reads=[tSn, "negoff"], writes=[tSn])
                            S.op("dve", lambda e: e.tensor_tensor(out=tS[:, ncol:ncol + 4], in0=ps2[0:4, 0:4], in1=Bn[:, gh, :], op=ALU.add), reads=[ps2n, "Bn"], writes=[tSn])
                            S.op("dve", lambda e: e.reduce_max(out=mlt[:, 0, gh:gh + 1], in_=tS[:, 0:ncol + 4], axis=AX.X), reads=[tSn], writes=["mlt"])
                            S.op("dve", lambda e: e.tensor_scalar(out=mlt[:, 3, gh:gh + 1], in0=mlt[:, 0, gh:gh + 1], scalar1=-1.0, scalar2=None, op0=ALU.mult), reads=["mlt"], writes=["mlt"])
                            eS, eSn = eSr.next()
                            S.op("act", lambda e: e.activation(out=eS[:, 0:ncol + 4], in_=tS[:, 0:ncol + 4], func=AF.Exp, bias=mlt[:, 3, gh:gh + 1], accum_out=mlt[:, 1, gh:gh + 1]),
                                 reads=[tSn, "mlt"], writes=[eSn, "mlt"])
                            sst[h] = (eS, eSn, ncol, gh)

                        def sB(h):
                            eS, eSn, ncol, gh = sst.pop(h)
                            pte, pten = pb.next()
                            for t in range(nblk):
                                S.op("pe", lambda e, t=t: e.transpose(out=pte[:, t * 4:(t + 1) * 4], in_=eS[:, t * 128:(t + 1) * 128], identity=ident_b[0:4, 0:4]), reads=[eSn, "ident_b"], writes=[pten])
                            S.op("pe", lambda e: e.transpose(out=pte[0:4, 16:20], in_=eS[:, ncol:ncol + 4], identity=ident_b[0:4, 0:4]), reads=[eSn, "ident_b"], writes=[pten])
                            eT, eTn = eTr.next()
                            S.op("dve", lambda e: e.tensor_copy(out=eT[:, 0:4 * nblk], in_=pte[:, 0:4 * nblk]), reads=[pten], writes=[eTn])
                            S.op("dve", lambda e: e.tensor_copy(out=eT[0:4, 16:20], in_=pte[0:4, 16:20]), reads=[pten, eTn], writes=[eTn])
                            po, pon = pf.next()
                            for t in range(nblk):
                                S.op("pe", lambda e, t=t: e.matmul(po[0:4, 0:128], lhsT=eT[:, t * 4:(t + 1) * 4], rhs=Vs[:, t, h * 128:(h + 1) * 128], start=(t == 0), stop=False), reads=[eTn, "Vs"], writes=[pon])
                            S.op("pe", lambda e: e.matmul(po[0:4, 0:128], lhsT=eT[0:4, 16:20], rhs=Vnb[:, h * 128:(h + 1) * 128], start=False, stop=True), reads=[eTn, "Vnb"], writes=[pon])
                            S.op("dve", lambda e: e.tensor_copy(out=Og[:, g, h, :], in_=po[0:4, 0:128]), reads=[pon], writes=["Og"])

                        sA(0)
                        sA(1)
                        for h in range(8):
                            if h + 2 < 8:
                                sA(h + 2)
                            sB(h)
                    chk("s_g")
                    m3 = mlt[:, 0, :].rearrange("p (g h) -> p g h", h=8)
                    l3 = mlt[:, 1, :].rearrange("p (g h) -> p g h", h=8)
                    w3 = mlt[:, 2, :].rearrange("p (g h) -> p g h", h=8)
                    Mx = mlt[:, 4, 0:8]
                    den = mlt[:, 5, 0:8]
                    S.op("dve", lambda e: e.tensor_tensor(out=Mx, in0=m3[:, 0, :], in1=m3[:, 1, :], op=ALU.max), reads=["mlt"], writes=["mlt"])
                    S.op("dve", lambda e: e.tensor_tensor(out=Mx, in0=Mx, in1=m3[:, 2, :], op=ALU.max), reads=["mlt"], writes=["mlt"])
                    S.op("dve", lambda e: e.tensor_tensor(out=w3, in0=m3, in1=Mx.unsqueeze(1).to_broadcast([4, 3, 8]), op=ALU.subtract), reads=["mlt"], writes=["mlt"])
                    S.op("act", lambda e: e.activation(out=mlt[:, 2, :], in_=mlt[:, 2, :], func=AF.Exp), reads=["mlt"], writes=["mlt"])
                    S.op("dve", lambda e: e.tensor_tensor(out=mlt[:, 3, :], in0=mlt[:, 2, :], in1=mlt[:, 1, :], op=ALU.mult), reads=["mlt"], writes=["mlt"])
                    S.op("dve", lambda e: e.tensor_tensor(out=den, in0=mlt[:, 3, 0:8], in1=mlt[:, 3, 8:16], op=ALU.add), reads=["mlt"], writes=["mlt"])
                    S.op("dve", lambda e: e.tensor_tensor(out=den, in0=den, in1=mlt[:, 3, 16:24], op=ALU.add), reads=["mlt"], writes=["mlt"])
                    S.op("dve", lambda e: e.reciprocal(out=den, in_=den), reads=["mlt"], writes=["mlt"])
                    S.op("dve", lambda e: e.tensor_tensor(out=w3, in0=w3, in1=den.unsqueeze(1).to_broadcast([4, 3, 8]), op=ALU.mult), reads=["mlt"], writes=["mlt"])
                    for g in range(3):
                        S.op("dve", lambda e, g=g: e.tensor_tensor(out=Og[:, g, :, :], in0=Og[:, g, :, :], in1=mlt[:, 2, g * 8:(g + 1) * 8].unsqueeze(2).to_broadcast([4, 8, 128]), op=ALU.mult),
                             reads=["Og", "mlt"], writes=["Og"])
                    S.op("dve", lambda e: e.tensor_tensor(out=osb[:], in0=Og[:, 0, :, :], in1=Og[:, 1, :, :], op=ALU.add), reads=["Og"], writes=["osb"])
                    S.op("dve", lambda e: e.tensor_tensor(out=osb[:], in0=osb[:], in1=Og[:, 2, :, :], op=ALU.add), reads=["Og", "osb"], writes=["osb"])
                    pso, pson = pf.next()
                    for h in range(8):
                        S.op("pe", lambda e, h=h: e.transpose(out=pso[:, h * 4:(h + 1) * 4], in_=osb[:, h, :], identity=ident_f[0:4, 0:4]), reads=["osb", "ident_f"], writes=[pson])
                    S.op("dve", lambda e: e.tensor_tensor(out=ogT[:, :, b * 128:b * 128 + 4], in0=pso[:, 0:32].rearrange("p (h t) -> p h t", t=4),
                                                          in1=gsils[:, :, b * 4:(b + 1) * 4], op=ALU.mult), reads=[pson, "gsils"], writes=["ogT"])
                chk("s_att")
                out_proj(T0s, 512, True)
        except _Stop:
            pass
        es3b.close()
        es3.close()
        es1.close()

        S.finish("sp")
    return nc


_PROG = {}


def _get_prog():
    if "nc" not in _PROG:
        _PROG["nc"] = build_program()
    return _PROG["nc"]


def _t5_bucket_np(dist):
    n = np.maximum(dist, 0)
    nf = np.maximum(n, 1).astype(np.float32)
    large = 16 + (np.log(nf / np.float32(16)) / np.float32(math.log(2048 / 16)) * np.float32(16)).astype(np.int32)
    large = np.minimum(large, 31)
    return np.where(n < 16, n, large)


def _onehot_const():
    oh = np.zeros((32, 3, 129), np.float32)
    for g, d in enumerate((1, 4, 16)):
        b = _t5_bucket_np(np.arange(129, dtype=np.int32) * d)
        oh[b, g, np.arange(129)] = 1.0
    return oh.reshape(32, 3 * 129)


def kernel(x_prompt, x_sample, state_ssm, state_conv, cache_kv, a_norm, a_w_in, a_conv_w, a_conv_b, a_dt_bias,
           a_A_log, a_D, a_gate_norm, a_w_out, rel_bias, kv_norm, w_kv, b_norm, b_w_in, b_w_out, final_norm):
    f = lambda a: np.ascontiguousarray(np.asarray(a, dtype=np.float32))
    shared = {
        "a_norm": f(a_norm[0]), "a_w_in": f(a_w_in[0]), "a_conv_w": f(a_conv_w[0]), "a_conv_b": f(a_conv_b[0]),
        "a_dt_bias": f(a_dt_bias[0]), "a_A_log": f(a_A_log[0]), "a_D": f(a_D[0]), "a_gate_norm": f(a_gate_norm[0]),
        "a_w_out": f(a_w_out[0]), "rel_bias": f(rel_bias), "kv_norm": f(kv_norm), "w_kv": f(w_kv),
        "b_norm": f(b_norm[0]), "b_w_in": f(b_w_in[0]), "b_w_out": f(b_w_out[0]), "final_norm": f(final_norm),
        "onehot": _onehot_const(),
    }
    in_maps = []
    for c in range(N_CORES):
        sl = slice(NSB * c, NSB * (c + 1))
        m = dict(shared)
        m["xp"] = f(x_prompt[c % 4])
        m["xs"] = f(np.asarray(x_sample)[sl].reshape(NSB * 4, D))
        m["sssm"] = f(np.asarray(state_ssm)[0, sl].reshape(NSB, DI, NS))
        m["sconv"] = f(np.asarray(state_conv)[0, sl])
        m["ckv"] = f(np.asarray(cache_kv)[sl].reshape(NSB, 2048, 2048))
        fl = np.zeros((128, 2), np.float32)
        fl[:, 0] = 1.0 if c >= 4 else 0.0
        fl[:, 1] = 0.0 if c >= 4 else -30000.0
        m["spanflag"] = fl
        in_maps.append(m)
    nc = _get_prog()
    res = run_bass_kernel_spmd(nc, in_maps, core_ids=list(range(N_CORES)))
    R = res.results
    y_prompt = np.stack([np.concatenate([R[b]["yp"], R[b + 4]["yp"]], 0) for b in range(4)], 0)
    y_sample = np.concatenate([R[c]["ys"].reshape(NSB, 4, D) for c in range(N_CORES)], 0)
    ssm_prompt = np.stack([R[b]["ssm_p"].reshape(NH, HP, NS) for b in range(4)], 0)[None]
    ssm_sample = np.concatenate([R[c]["ssm_s"].reshape(NSB, NH, HP, NS) for c in range(N_CORES)], 0)[None]
    conv_prompt = np.stack([R[b]["conv_p"] for b in range(4)], 0)[None]
    conv_sample = np.concatenate([R[c]["conv_s"] for c in range(N_CORES)], 0)[None]
    kv_prompt = np.stack([R[b]["kv_p"].reshape(2048, 2, 8, 128) for b in range(4)], 0)
    kv_sample = np.concatenate([R[c]["kv_s"].reshape(NSB, 4, 2, 8, 128) for c in range(N_CORES)], 0)
    return (y_prompt, y_sample, ssm_prompt, ssm_sample, conv_prompt, conv_sample, kv_prompt, kv_sample)
```
